# Optimizing a Trainium2 kernel written in Bass

```python
import jax, jax.numpy as jnp
from jax import lax
import numpy as np

D_MODEL = 2048
BATCH = 4
SEQ = 2048
DEPTH = 2
DEC_BATCH = 128
DEC_SEQ = 1
PAST_LEN = 16384
PAGE_SIZE = 128

D_MIX = D_MODEL
D_ML = D_MIX // 2
D_CM = D_MIX - D_ML
ML_HEADS = 4
ML_HD = D_ML // ML_HEADS
CM_GROUPS = 4
CM_GD = D_CM // CM_GROUPS
CM_CHUNK = 128
ML_CHUNK = 128
D_IN = 5 * D_ML + 2 * ML_HEADS + 3 * D_CM
EPS = 1e-6

kernel_name = 'hymba_mlstm_chunkmlp_decoder_step'


def rmsnorm(x, g):
    xf = x.astype(jnp.float32)
    y = xf * lax.rsqrt(jnp.mean(xf * xf, axis=-1, keepdims=True) + EPS)
    return (y * g.astype(jnp.float32)).astype(x.dtype)


def layernorm(x, g):
    xf = x.astype(jnp.float32)
    mu = jnp.mean(xf, axis=-1, keepdims=True)
    xc = xf - mu
    y = xc * lax.rsqrt(jnp.mean(xc * xc, axis=-1, keepdims=True) + EPS)
    return (y * g.astype(jnp.float32)).astype(x.dtype)


def mlstm_chunk(carry, inp):
    C, n, m = carry
    q, k, v, ig, lf = inp
    L = q.shape[2]
    b = jnp.cumsum(lf, axis=-1)
    causal = jnp.tril(jnp.ones((L, L), dtype=bool))
    D = jnp.where(causal, b[..., :, None] - b[..., None, :] + ig[..., None, :], -jnp.inf)
    a = b + m[..., None]
    mt = jnp.maximum(a, jnp.max(D, axis=-1))
    w_intra = jnp.exp(D - mt[..., None])
    w_inter = jnp.exp(a - mt)
    s = jnp.einsum('bhtd,bhsd->bhts', q, k) * w_intra
    num = w_inter[..., None] * jnp.einsum('bhvd,bhtd->bhtv', C, q) + jnp.einsum('bhts,bhsv->bhtv', s, v)
    den = w_inter * jnp.einsum('bhd,bhtd->bht', n, q) + jnp.sum(s, axis=-1)
    h = num / jnp.maximum(jnp.abs(den), jnp.exp(-mt))[..., None]
    m_new = mt[..., -1]
    g_end = jnp.exp(b[..., -1:] - b + ig - m_new[..., None])
    decay = jnp.exp(a[..., -1] - m_new)
    C_new = decay[..., None, None] * C + jnp.einsum('bhs,bhsv,bhsd->bhvd', g_end, v, k)
    n_new = decay[..., None] * n + jnp.einsum('bhs,bhsd->bhd', g_end, k)
    return (C_new, n_new, m_new), h


def mlstm(q, k, v, ig, lf, C0, n0, m0):
    B, T, H, d = q.shape
    L = min(ML_CHUNK, T)
    nc = T // L
    c4 = lambda t: t.reshape(B, nc, L, H, t.shape[-1]).transpose(1, 0, 3, 2, 4)
    c3 = lambda t: t.reshape(B, nc, L, H).transpose(1, 0, 3, 2)
    (C1, n1, m1), hs = lax.scan(mlstm_chunk, (C0, n0, m0), (c4(q), c4(k), c4(v), c3(ig), c3(lf)))
    h = hs.transpose(1, 0, 3, 2, 4).reshape(B, T, H, hs.shape[-1])
    return h, C1, n1, m1


def spatial_mix(v, w_s, b_s):
    B, T, G, dg = v.shape
    L = min(CM_CHUNK, T)
    nc = T // L
    w = jnp.where(jnp.tril(jnp.ones((L, L), dtype=bool)), w_s[:, :L, :L], 0)
    vc = v.reshape(B, nc, L, G, dg)
    out = jnp.einsum('gts,bcsgd->bctgd', w, vc) + b_s[:, :L].T[None, None, :, :, None]
    return out.reshape(B, T, G, dg)


def mixer_layer(x, c, C0, n0, m0, g_norm, w_ada, b_ada, w_in, b_igate, b_fgate, g_mh, g_cmv, w_s, b_s, w_out):
    B, T, _ = x.shape
    f32 = jnp.float32
    shift, scale, gate = jnp.split(jax.nn.silu(c) @ w_ada + b_ada, 3, axis=-1)
    h = rmsnorm(x, g_norm) * (1 + scale[:, None, :]) + shift[:, None, :]
    p = h @ w_in
    sizes = [D_ML] * 5 + [ML_HEADS] * 2 + [D_CM] * 3
    offs = [int(o) for o in np.cumsum(sizes)[:-1]]
    q, k, v, o, z_ml, i_pre, f_pre, u, v_cm, z_cm = jnp.split(p, offs, axis=-1)
    heads = lambda t: t.reshape(B, T, ML_HEADS, ML_HD)
    ig = (i_pre + b_igate).astype(f32)
    lf = jax.nn.log_sigmoid((f_pre + b_fgate).astype(f32))
    h_ml, C1, n1, m1 = mlstm(heads(q).astype(f32), heads(k).astype(f32) * (ML_HD ** -0.5),
                             heads(v).astype(f32), ig, lf,
                             C0.astype(f32), n0.astype(f32), m0.astype(f32))
    h_ml = jax.nn.sigmoid(heads(o)) * h_ml.astype(x.dtype)
    h_ml = rmsnorm(h_ml, g_mh.reshape(ML_HEADS, ML_HD)).reshape(B, T, D_ML) * jax.nn.silu(z_ml)
    u = jax.nn.gelu(u)
    v_n = layernorm(jax.nn.gelu(v_cm).reshape(B, T, CM_GROUPS, CM_GD), g_cmv.reshape(CM_GROUPS, CM_GD))
    h_cm = u * spatial_mix(v_n, w_s, b_s).reshape(B, T, D_CM) * jax.nn.silu(z_cm)
    out = jnp.concatenate([h_ml, h_cm], axis=-1) @ w_out
    x = x + gate[:, None, :] * out
    return x, C1, n1, m1, v_n.reshape(B, T, D_CM)


def setup_inputs(seed: int = 0) -> dict:
    key = jax.random.key(seed)
    ks = jax.random.split(key, 24)
    nrm = jax.random.normal
    f32 = jnp.float32
    b_f = jnp.linspace(3.0, 6.0, ML_HEADS, dtype=f32)[None, :] + 0.1 * nrm(ks[12], (DEPTH, ML_HEADS), f32)
    return {
        'x_prompt': nrm(ks[0], (BATCH, SEQ, D_MODEL), f32),
        'x_sample': nrm(ks[1], (DEC_BATCH, DEC_SEQ, D_MODEL), f32),
        'state_C': 0.02 * nrm(ks[2], (DEPTH, DEC_BATCH, ML_HEADS, ML_HD, ML_HD), f32),
        'state_n': 0.02 * nrm(ks[3], (DEPTH, DEC_BATCH, ML_HEADS, ML_HD), f32),
        'state_m': jax.random.uniform(ks[4], (DEPTH, DEC_BATCH, ML_HEADS), f32, 0.0, 2.0),
        'c_prompt': nrm(ks[5], (BATCH, D_MODEL), f32),
        'c_sample': nrm(ks[6], (DEC_BATCH, D_MODEL), f32),
        'g_norm': 1.0 + 0.02 * nrm(ks[7], (DEPTH, D_MODEL), f32),
        'w_ada': 0.5 * D_MODEL ** -0.5 * nrm(ks[8], (DEPTH, D_MODEL, 3 * D_MODEL), f32),
        'b_ada': 0.02 * nrm(ks[9], (DEPTH, 3 * D_MODEL), f32),
        'w_in': D_MODEL ** -0.5 * nrm(ks[10], (DEPTH, D_MODEL, D_IN), f32),
        'b_igate': 0.1 * nrm(ks[11], (DEPTH, ML_HEADS), f32),
        'b_fgate': b_f,
        'g_mh': 1.0 + 0.02 * nrm(ks[13], (DEPTH, D_ML), f32),
        'g_cmv': 1.0 + 0.02 * nrm(ks[14], (DEPTH, D_CM), f32),
        'w_s': CM_CHUNK ** -0.5 * nrm(ks[15], (DEPTH, CM_GROUPS, CM_CHUNK, CM_CHUNK), f32),
        'b_s': 1.0 + 0.1 * nrm(ks[16], (DEPTH, CM_GROUPS, CM_CHUNK), f32),
        'w_out': D_MIX ** -0.5 * nrm(ks[17], (DEPTH, D_MIX, D_MODEL), f32),
        'g_final': 1.0 + 0.02 * nrm(ks[18], (D_MODEL,), f32),
    }


def reference(x_prompt, x_sample, state_C, state_n, state_m, c_prompt, c_sample,
              g_norm, w_ada, b_ada, w_in, b_igate, b_fgate, g_mh, g_cmv, w_s, b_s, w_out, g_final):
    f32 = jnp.float32
    zC = jnp.zeros((BATCH, ML_HEADS, ML_HD, ML_HD), f32)
    zn = jnp.zeros((BATCH, ML_HEADS, ML_HD), f32)
    zm = jnp.zeros((BATCH, ML_HEADS), f32)
    yp, ys = x_prompt, x_sample
    Cp_l, np_l, mp_l, Cs_l, ns_l, ms_l, vs_l = [], [], [], [], [], [], []
    for l in range(DEPTH):
        lp = (g_norm[l], w_ada[l], b_ada[l], w_in[l], b_igate[l], b_fgate[l],
              g_mh[l], g_cmv[l], w_s[l], b_s[l], w_out[l])
        yp, Cp, np_, mp, _ = mixer_layer(yp, c_prompt, zC, zn, zm, *lp)
        ys, Cs, ns, ms, vs = mixer_layer(ys, c_sample, state_C[l], state_n[l], state_m[l], *lp)
        Cp_l.append(Cp); np_l.append(np_); mp_l.append(mp)
        Cs_l.append(Cs); ns_l.append(ns); ms_l.append(ms); vs_l.append(vs)
    y_prompt = rmsnorm(yp, g_final)
    y_sample = rmsnorm(ys, g_final)
    return (y_prompt, y_sample,
            jnp.stack(Cp_l), jnp.stack(np_l), jnp.stack(mp_l),
            jnp.stack(Cs_l), jnp.stack(ns_l), jnp.stack(ms_l), jnp.stack(vs_l))
```

```python
import numpy as np
from contextlib import ExitStack
import concourse.bass as bass
import concourse.mybir as mybir
from concourse.bass_utils import run_bass_kernel_spmd

F32 = mybir.dt.float32
BF16 = mybir.dt.bfloat16
AF = mybir.ActivationFunctionType
ALU = mybir.AluOpType
AX = mybir.AxisListType

D = 2048
KC = 16
NHD = 4
HD = 256
NS = 16
EPS = 1e-6
NHALF = 1
COLL = (NHALF == 1)
RG = [[0, 1], [2, 3], [4, 5], [6, 7]]
NT = NHALF * 1024
NCORES = 8


import os
KSTOP = float(os.environ.get('KSTOP', '99'))


class StopBuild(Exception):
    pass


def stage(n):
    if n >= KSTOP:
        raise StopBuild()


class Tl:
    def __init__(self, ap, name, bank=None):
        self.ap = ap
        self.name = name
        self.w = {}
        self.r = {}
        self.dsem = None
        self.bank = bank

    def __getitem__(self, idx):
        return self.ap[idx]


class K:
    def __init__(self, nc, stack):
        self.nc = nc
        self.stack = stack
        self.engs = {"pe": nc.tensor, "dve": nc.vector, "act": nc.scalar, "pool": nc.gpsimd, "sp": nc.sync}
        self.sem = {}
        self.cnt = {}
        self.seen = {e: {} for e in self.engs}
        for e in self.engs:
            self.sem[e] = stack.enter_context(nc.semaphore("s_" + e))
            self.cnt[e] = 0
        self.nsem = 0
        self.dsems = []
        self.pe_reads = []
        self.pe_writes = []
        self.flip = 0

    def sb(self, name, shape, dt):
        return Tl(self.stack.enter_context(self.nc.sbuf_tensor(name, shape, dt)), name)

    def ps(self, name, shape, dt):
        t = Tl(self.stack.enter_context(self.nc.psum_tensor(name, shape, dt)), name)
        t.bank = t
        return t

    def _wait(self, e, tok):
        if tok is None:
            return
        s, v = tok
        key = id(s)
        if self.seen[e].get(key, 0) >= v:
            return
        self.seen[e][key] = v
        self.engs[e].wait_ge(s, v)

    @staticmethod
    def _split(reads, writes):
        rr, ww = [], []
        for t in reads:
            if t is None:
                continue
            if t.bank is not None:
                if t.bank not in ww:
                    ww.append(t.bank)
            elif t not in rr:
                rr.append(t)
        for t in writes:
            if t is None:
                continue
            t = t.bank if t.bank is not None else t
            if t not in ww:
                ww.append(t)
        return rr, ww

    def deps(self, e, reads, writes):
        reads, writes = self._split(reads, writes)
        for t in reads:
            for tok in t.w.values():
                self._wait(e, tok)
        for t in writes:
            for tok in t.w.values():
                self._wait(e, tok)
            for tok in t.r.values():
                self._wait(e, tok)

    def reg(self, tok, reads, writes):
        reads, writes = self._split(reads, writes)
        sid = id(tok[0])
        for t in reads:
            if t.r.get(sid, (None, 0))[1] < tok[1]:
                t.r[sid] = tok
        for t in writes:
            if t.w.get(sid, (None, 0))[1] < tok[1]:
                t.w[sid] = tok

    def op(self, e, fn, reads, writes):
        self.deps(e, reads, writes)
        ins = fn()
        self.cnt[e] += 1
        ins.then_inc(self.sem[e], 1)
        tok = (self.sem[e], self.cnt[e])
        self.reg(tok, reads, writes)
        return tok

    def mm(self, out_tl, out_ap, lhsT_tl, lhsT_ap, rhs_tl, rhs_ap, start, stop, last=None, transpose=False):
        reads = [t for t in (lhsT_tl, rhs_tl) if t is not None]
        self.deps("pe", reads, [out_tl] if start else [])
        if transpose:
            ins = self.nc.tensor.transpose(out_ap, lhsT_ap, rhs_ap)
        else:
            ins = self.nc.tensor.matmul(out_ap, lhsT_ap, rhs_ap, start=start, stop=stop)
        self.pe_reads += reads
        if out_tl not in self.pe_writes:
            self.pe_writes.append(out_tl)
        if last is None:
            last = stop
        if last:
            self.cnt["pe"] += 1
            ins.then_inc(self.sem["pe"], 1)
            tok = (self.sem["pe"], self.cnt["pe"])
            self.reg(tok, self.pe_reads, self.pe_writes)
            self.pe_reads = []
            self.pe_writes = []
            return tok
        return None

    def dma(self, e, out_ap, in_ap, reads, writes, pwrites=(), **kw):
        pw = list(pwrites)
        reads = [t for t in reads if t is not None]
        self.deps(e, reads, writes)
        cands = [t for t in list(writes) if t.ap is not None] or [t for t in reads if t.ap is not None] or (list(writes) + pw)
        owner = cands[0]
        if owner.dsem is None:
            self.nsem += 1
            owner.dsem = [self.stack.enter_context(self.nc.semaphore("d%d" % self.nsem)), 0]
            self.dsems.append(owner.dsem)
        ins = self.engs[e].dma_start(out=out_ap, in_=in_ap, **kw)
        owner.dsem[1] += 16
        ins.then_inc(owner.dsem[0], 16)
        tok = (owner.dsem[0], owner.dsem[1])
        self.reg(tok, reads, list(writes) + pw)
        return tok

    def cc(self, in_ap, out_ap, reads, writes):
        self.deps("pool", reads, writes)
        if not hasattr(self, "ccsem"):
            self.ccsem = [self.stack.enter_context(self.nc.semaphore("ccsem")), 0]
        ins = self.nc.gpsimd.collective_compute("AllGather", ALU.bypass, replica_groups=RG, ins=[in_ap.opt()], outs=[out_ap.opt()])
        self.ccsem[1] += 1
        ins.then_inc(self.ccsem[0])
        tok = (self.ccsem[0], self.ccsem[1])
        self.reg(tok, reads, writes)
        return tok

    def ve(self):
        self.flip ^= 1
        return "dve" if self.flip else "act"


def build_nc():
    nc = bass.Bass("TRN2", target_bir_lowering=False)

    def din(name, shape):
        return nc.dram_tensor(name, shape, F32, kind="ExternalInput").ap()

    def dout(name, shape):
        return nc.dram_tensor(name, shape, F32, kind="ExternalOutput").ap()

    xp = din("xp", [NT, D])
    xsm = din("xsm", [NS, D])
    cvec = din("cvec", [17, D])
    sC = din("sC", [2, NS, NHD, 128, 2, HD])
    sn = din("sn", [2, NS, NHD * HD])
    smm = din("smm", [2, NS, NHD])
    w_ada = din("w_ada", [2, D, 6144])
    w_inp = din("w_inp", [2, D, 8192])
    w_out = din("w_out", [2, D, D])
    wgT = din("wgT", [2, 128, KC, 8])
    badaT = din("badaT", [2, 128, 32])
    bgate = din("bgate", [2, D])
    gnT = din("gnT", [2, 128, KC])
    big = din("big", [2, 8])
    gmh = din("gmh", [2, 1024])
    gcmv = din("gcmv", [2, 1024])
    wsT = din("wsT", [2, 4, 128, 128])
    bsT = din("bsT", [2, 128, 4])
    ws00 = din("ws00", [2, 8])
    gfin = din("gfin", [1, D])
    consts = din("consts", [5, 128, 128])
    msk = din("msk", [1, 1])

    yp = dout("yp", [NT, D])
    ys = dout("ys", [NS, D])
    Cp = dout("Cp", [2, NHD, 128, 2, HD])
    npo = dout("npo", [2, NHD, 128, 2])
    mpo = dout("mpo", [2, NHD])
    Cs = dout("Cs", [2, NS, NHD, 128, 2, HD])
    nso = dout("nso", [2, NS, NHD * HD])
    mso = dout("mso", [2, NS, NHD])
    vro = dout("vro", [2, NS, 1024])

    xs1 = nc.dram_tensor("xs1", [NT + NS, D], F32, kind="Internal").ap()
    xs2 = nc.dram_tensor("xs2", [NT + NS, D], F32, kind="Internal").ap()
    cinC = [[nc.dram_tensor("cinC_%d_%d" % (l, h), [128, 514], F32).ap() for h in range(NHD)] for l in range(2)]
    coutC = [[nc.dram_tensor("coutC_%d_%d" % (l, h), [256, 514], F32).ap() for h in range(NHD)] for l in range(2)]
    cinM = [nc.dram_tensor("cinM_%d" % l, [1, 4], F32).ap() for l in range(2)]
    qks = nc.dram_tensor("qks", [2, NHD, NS, 512], F32).ap()
    coutM = [nc.dram_tensor("coutM_%d" % l, [2, 4], F32).ap() for l in range(2)]

    with ExitStack() as st:
        k = K(nc, st)
        V, A, PE_ = nc.vector, nc.scalar, nc.tensor

        NRING = 3
        wring = [k.sb("wr%d" % i, [128, KC, 512], BF16) for i in range(NRING)]
        hT = k.sb("hT", [128, KC, 1152], BF16)
        mT = k.sb("mT", [128, KC, 1152], BF16)
        qT = k.sb("qT", [128, 2, 1024], BF16)
        kT = k.sb("kT", [128, 2, 1024], BF16)
        ktok = k.sb("ktok", [128, 8, 256], BF16)
        vtok = k.sb("vtok", [128, 8, 256], BF16)
        ogt = st.enter_context(nc.sbuf_tensor("og", [128, 9, 256], F32))
        og = [Tl(ogt[:, i, :], "og%d" % i) for i in range(9)]
        xtile_ap = ogt[:, 0:8, :].rearrange("p a b -> p (a b)")
        class XB:
            def __init__(self, parts, tls):
                self.parts = parts
                self.tls = tls

            def sl(self, P, lo, hi):
                for ap, off in self.parts:
                    w = ap.shape[1]
                    if off <= lo and hi <= off + w:
                        return ap[0:P, lo - off:hi - off]
                raise AssertionError((lo, hi))

        f32v = lambda t: t[:, :, :].rearrange("p a b -> p (a b)").bitcast(F32)
        xbs = [XB([(xtile_ap, 0)], og[0:8]),
               XB([(f32v(qT), 0), (f32v(kT), 1024)], [qT, kT]),
               XB([(f32v(ktok), 0), (f32v(vtok), 1024)], [ktok, vtok])]
        rr = k.sb("rr", [128, 9], F32)
        ssc = k.sb("ssc", [128, 8], F32)
        ssc2 = k.sb("ssc2", [128, 8], F32)
        CT32 = [k.sb("CT%d" % h, [128, 2, 256], F32) for h in range(NHD)]
        n32 = [k.sb("n32_%d" % h, [128, 2], F32) for h in range(NHD)]
        CTb = k.sb("CTb", [128, 2, 256], BF16)
        nb = k.sb("nb", [128, 2], BF16)
        cst = k.sb("cst", [128, 5, 128], F32)
        identb = k.sb("identb", [128, 128], BF16)
        onesb = k.sb("onesb", [128, 1], BF16)
        scT = k.sb("scT", [128, KC, 17], BF16)
        scTb = k.sb("scTb", [128, KC, 128], BF16)
        scTs = k.sb("scTs", [128, KC, 128], BF16)
        wg = k.sb("wg", [128, 2, KC, 8], BF16)
        adaT = k.sb("adaT", [128, 32, 17], F32)
        adaT2 = k.sb("adaT2", [128, 32, 17], F32)
        badaT_sb = k.sb("badaT_sb", [128, 2, 32], F32)
        gnT_sb = k.sb("gnT_sb", [128, 2, KC], F32)
        gsT = k.sb("gsT", [128, KC], F32)
        shP = k.sb("shP", [128, KC], F32)
        sc1T = k.sb("sc1T", [128, KC, NS], F32)
        shS = k.sb("shS", [128, KC, NS], F32)
        big_bc = k.sb("big_bc", [128, 2, 8], F32)
        ws00_bc = k.sb("ws00_bc", [128, 2, 8], F32)
        bsT_sb = k.sb("bsT_sb", [128, 2, 4], F32)
        wsTb = k.sb("wsTb", [128, 4, 128], BF16)
        gmh_bc = k.sb("gmh_bc", [128, 256], F32)
        gcmv_bc = k.sb("gcmv_bc", [128, 256], F32)
        gate_bc = k.sb("gate_bc", [128, 512], F32)
        gate_s = k.sb("gate_s", [128, 512], F32)
        wk = [k.sb("wk%d" % i, [128, 512], F32) for i in range(6)]
        wkb = [k.sb("wkb%d" % i, [128, 512], BF16) for i in range(2)]
        ssq4 = k.sb("ssq4", [128, 4], F32)
        bst = k.sb("bst", [128, 24], F32)
        sm1 = [k.sb("sm1_%d" % i, [128, 16], F32) for i in range(8)]
        gnames = ["IG", "XF", "LL", "BL", "U", "CU", "CUl", "BLl", "M", "WI", "ENM", "G", "DEC", "Ml", "T1", "GA", "DECA", "MlA", "NEGM"]
        gt = {n: k.sb("g_" + n, [128, 8, 4], F32) for n in gnames}
        mbc = k.sb("mbc", [128, 9, 4], F32)
        mbcA = k.sb("mbcA", [128, 9, 4], F32)
        msk_bc = k.sb("msk_bc", [128, 1], F32)
        min_bc = k.sb("min_bc", [128, 4], F32)
        UT = wk[0]
        CUT = wk[1]
        qk_s = k.sb("qk_s", [NS, 512], F32)
        v_s = k.sb("v_s", [NS, 256], F32)
        cq_tok = k.sb("cq_tok", [NS, 256], F32)
        n_s = k.sb("n_s", [NS, 256], F32)
        sg = {n: k.sb("sg_" + n, [NS, 4], F32) for n in ["ig", "xf", "ll", "a", "mt", "wa", "wi", "enm", "m0", "t"]}
        ssm = [k.sb("ssm%d" % i, [NS, 4], F32) for i in range(6)]
        dgw = k.sb("dgw", [NS, 2, NS], F32)
        wbc = k.sb("wbc", [128, 2, NS], F32)
        vsT = k.sb("vsT", [128, 2, NS], F32)
        cqT = k.sb("cqT", [128, 2, NS], F32)
        Cbuf = [k.sb("Cbuf%d" % i, [128, 2, 256], F32) for i in range(2)]
        pj = [k.ps("pj%d" % i, [128, 512], F32) for i in range(2)]
        ptb = k.ps("ptb", [128, 1024], BF16)
        pt32 = k.ps("pt32", [128, 512], F32)
        pm1t = st.enter_context(nc.psum_tensor("pm1", [128, 512], F32))
        bank_pm1 = Tl(None, "bank_pm1")
        pS = Tl(pm1t[:, 0:128], "pS", bank=bank_pm1)
        pbc = Tl(pm1t[:, 128:256], "pbc", bank=bank_pm1)
        psml = Tl(pm1t[:, 256:512], "psml", bank=bank_pm1)
        pnumt = st.enter_context(nc.psum_tensor("pnum", [128, 512], F32))
        bank_pnum = Tl(None, "bank_pnum")
        pinter = Tl(pnumt[:, 0:256], "pinter", bank=bank_pnum)
        pintra = Tl(pnumt[:, 256:512], "pintra", bank=bank_pnum)
        pupd = k.ps("pupd", [128, 2, 256], F32)
        psb = k.ps("psb", [128, 512], F32)
        pupd512 = Tl(pupd[:, :, :].rearrange("p a b -> p (a b)"), "pupd512", bank=pupd)

        ident = cst[:, 0, :]
        ones = cst[:, 1, :]
        tri = cst[:, 2, :]
        maskneg = cst[:, 3, :]
        e127 = cst[:, 4, :]

        def dtl(name):
            return Tl(None, name)
        xs_t = {}
        for nm in ("xs1", "xs2"):
            for tt in range(NT // 128 + 1):
                xs_t[(nm, tt)] = dtl("%s_%d" % (nm, tt))
        cinM_t = [dtl("cinM%d" % l) for l in range(2)]
        coutM_t = [dtl("coutM%d" % l) for l in range(2)]
        cinC_t = [[dtl("cinC") for h in range(NHD)] for l in range(2)]
        coutC_t = [[dtl("coutC") for h in range(NHD)] for l in range(2)]
        out_t = {n: dtl(n) for n in ["yp", "ys", "Cp", "npo", "mpo", "Cs", "nso", "mso", "vro"]}

        EARLY_ADA = COLL
        units = []
        for l in range(2):
            for half in range(NHALF):
                if half == 0 and not (EARLY_ADA and l == 1):
                    for u in range(8):
                        units.append(("ada", l, u))
                for i in range(16):
                    units.append(("inp", l, i))
                    if EARLY_ADA and l == 0 and i % 4 == 3:
                        units.append(("ada", 1, 2 * (i // 4)))
                        units.append(("ada", 1, 2 * (i // 4) + 1))
                for j in range(4):
                    units.append(("ada", l, 8 + j))
                    units.append(("out", l, j))
        wstate = {"loaded": 0, "next": 0}

        def wsrc(u):
            kind, l, b = u
            src = {"ada": w_ada, "inp": w_inp, "out": w_out}[kind]
            return src[l, :, b * 512:(b + 1) * 512].rearrange("(kc p) c -> p kc c", p=128)

        def wload(j):
            slot = wring[j % NRING]
            k.dma("pool", slot[:, :, :], wsrc(units[j]), [], [slot])

        def wacquire(expect):
            j = wstate["next"]
            assert units[j] == expect, (units[j], expect)
            while wstate["loaded"] < min(len(units), j + NRING) and wstate["loaded"] <= j:
                wload(wstate["loaded"])
                wstate["loaded"] += 1
            wstate["next"] += 1
            return wring[j % NRING], j

        def wrelease(j):
            nxt = j + NRING
            if nxt < len(units) and wstate["loaded"] == nxt:
                wload(nxt)
                wstate["loaded"] += 1

        for j in range(NRING):
            wload(j)
        wstate["loaded"] = NRING

        k.dma("sp", cst[:, :, :], consts.rearrange("c p f -> p c f"), [], [cst])
        k.op("dve", lambda: V.tensor_copy(out=identb[:, :], in_=ident), [cst], [identb])
        k.op("dve", lambda: V.memset(onesb[:, :], 1.0), [], [onesb])
        k.dma("sp", msk_bc[:, :], msk[0].partition_broadcast(128), [], [msk_bc])
        wgf = wk[0]
        k.dma("sp", wgf[:, 0:256].rearrange("p (l k c) -> p l k c", l=2, k=KC), wgT.rearrange("l p k c -> p l k c"), [], [wgf])
        k.op("dve", lambda: V.tensor_copy(out=wg[:, :, :, :], in_=wgf[:, 0:256].rearrange("p (l k c) -> p l k c", l=2, k=KC)), [wgf], [wg])
        k.dma("sp", badaT_sb[:, :, :], badaT.rearrange("l p c -> p l c"), [], [badaT_sb])
        k.dma("sp", gnT_sb[:, :, :], gnT.rearrange("l p c -> p l c"), [], [gnT_sb])
        k.dma("sp", bsT_sb[:, :, :], bsT.rearrange("l p c -> p l c"), [], [bsT_sb])
        for l in range(2):
            k.dma("sp", big_bc[:, l, :], big[l].partition_broadcast(128), [], [], pwrites=[big_bc])
            k.dma("sp", ws00_bc[:, l, :], ws00[l].partition_broadcast(128), [], [], pwrites=[ws00_bc])
        cv = wk[1]
        cvt = Tl(None, "cvt")
        for q4 in range(4):
            k.dma("sp", cv[0:17, :], cvec[:, q4 * 512:(q4 + 1) * 512], [], [cv])
            k.op("act", lambda: A.activation(out=cv[0:17, :], in_=cv[0:17, :], func=AF.Silu), [cv], [cv])
            for j in range(4):
                k.mm(pt32, pt32[:, j * 17:(j + 1) * 17], cv, cv[0:17, j * 128:(j + 1) * 128], cst, ident[0:17, 0:17],
                     start=True, stop=True, last=(j == 3), transpose=True)
            k.op("dve", lambda: V.tensor_copy(out=scT[:, q4 * 4:(q4 + 1) * 4, :],
                                               in_=pt32[:, 0:68].rearrange("p (a b) -> p a b", a=4)), [pt32], [scT])
        k.op("dve", lambda: V.tensor_copy(out=scTb[:, :, :], in_=scT[:, :, 0:1].to_broadcast([128, KC, 128])), [scT], [scTb])
        k.op("dve", lambda: V.memset(scTs[:, :, :], 0.0), [], [scTs])
        k.op("dve", lambda: V.tensor_copy(out=scTs[:, :, 0:NS], in_=scT[:, :, 1:17]), [scT], [scTs])
        k.op("dve", lambda: V.memset(hT[:, :, 1024:1152], 0.0), [], [hT])
        k.op("dve", lambda: V.memset(mT[:, :, 1024:1152], 0.0), [], [mT])

        def rstd_from_ss(P, ss_ap, ss_tl, n, out_tl, out_ap, tmp_tl, tmp_ap):
            k.op("dve", lambda: V.tensor_scalar(out=tmp_ap, in0=ss_ap, scalar1=1.0 / n, scalar2=EPS, op0=ALU.mult, op1=ALU.add),
                 [ss_tl], [tmp_tl])
            k.op("act", lambda: A.activation(out=tmp_ap, in_=tmp_ap, func=AF.Sqrt), [tmp_tl], [tmp_tl])
            k.op("dve", lambda: V.reciprocal(out=out_ap, in_=tmp_ap), [tmp_tl], [out_tl])

        def proj_tok(unit, tt, ntok, tok0, pjt):
            for kc in range(KC):
                k.mm(pjt, pjt[:, :], hT, hT[:, kc, tok0:tok0 + 128], unit, unit[:, kc, :], start=(kc == 0), stop=(kc == KC - 1))

        tiles_of_half = lambda half: [(c, 128, c * 128) for c in range(8)] + ([(8, NS, 1024)] if half == 0 else [])

        try:
          stage(0)
          for l in range(2):
              xin = (lambda tt, half: (xp[half * 1024 + tt * 128: half * 1024 + (tt + 1) * 128, :] if tt < 8 else xsm)) if l == 0 else \
                    (lambda tt, half: (xs1[half * 1024 + tt * 128: half * 1024 + (tt + 1) * 128, :] if tt < 8 else xs1[NT:NT + NS, :]))
              xin_t = (lambda tt, half: None) if l == 0 else (lambda tt, half: xs_t[("xs1", half * 8 + tt if tt < 8 else NT // 128)])
              xout = xs1 if l == 0 else xs2
              xout_nm = "xs1" if l == 0 else "xs2"

              for h in range(NHD):
                  k.op("dve", lambda: V.memset(CT32[h][:, :, :], 0.0), [], [CT32[h]])
                  k.op("dve", lambda: V.memset(n32[h][:, :], 0.0), [], [n32[h]])
              k.op("dve", lambda: V.memset(mbc[:, 8, :], 0.0), [], [mbc])

              for half in range(NHALF):
                  tiles = tiles_of_half(half)
                  def ada_unit(l_, u, dst):
                      unit, uj = wacquire(("ada", l_, u))
                      for j in range(4):
                          for kc in range(KC):
                              k.mm(pt32, pt32[:, j * 17:(j + 1) * 17], unit, unit[:, kc, j * 128:(j + 1) * 128], scT, scT[:, kc, :],
                                   start=(kc == 0), stop=(kc == KC - 1), last=(kc == KC - 1 and j == 3))
                      wrelease(uj)
                      k.op("dve", lambda: V.tensor_tensor(out=dst[:, u * 4:(u + 1) * 4, :],
                                                          in0=pt32[:, 0:68].rearrange("p (a b) -> p a b", a=4),
                                                          in1=badaT_sb[:, l_, u * 4:(u + 1) * 4].unsqueeze(2).to_broadcast([128, 4, 17]),
                                                          op=ALU.add), [pt32, badaT_sb], [dst])

                  if half == 0:
                      if EARLY_ADA and l == 1:
                          adaS = adaT2
                      else:
                          adaS = adaT
                          for u in range(8):
                              ada_unit(l, u, adaT)
                      k.op("dve", lambda: V.scalar_tensor_tensor(out=gsT[:, :], in0=adaS[:, 16:32, 0], scalar=1.0, in1=gnT_sb[:, l, :],
                                                                 op0=ALU.add, op1=ALU.mult), [adaS, gnT_sb], [gsT])
                      k.op("dve", lambda: V.tensor_copy(out=shP[:, :], in_=adaS[:, 0:16, 0]), [adaS], [shP])
                      k.op("dve", lambda: V.scalar_tensor_tensor(out=sc1T[:, :, :], in0=adaS[:, 16:32, 1:17], scalar=1.0,
                                                                 in1=gnT_sb[:, l, :].unsqueeze(2).to_broadcast([128, KC, NS]),
                                                                 op0=ALU.add, op1=ALU.mult), [adaS, gnT_sb], [sc1T])
                      k.op("dve", lambda: V.tensor_copy(out=shS[:, :, :], in_=adaS[:, 0:16, 1:17]), [adaS], [shS])

                  stage(1)
                  def xload(ti):
                      tt_, P_, tok0_ = tiles[ti]
                      xb_ = xbs[ti % 3]
                      src_t_ = xin_t(tt_, half)
                      for ap_, off_ in xb_.parts:
                          w_ = ap_.shape[1]
                          k.dma("sp", ap_[0:P_, :], xin(tt_, half)[:, off_:off_ + w_], [src_t_] if src_t_ else [], xb_.tls)

                  xload(0)
                  xload(1)
                  for ti, (tt, P, tok0) in enumerate(tiles):
                      if ti + 2 < len(tiles):
                          xload(ti + 2)
                      xb = xbs[ti % 3]
                      cover = xb.tls
                      ssq = sm1[0]
                      for j4 in range(4):
                          k.op("dve", lambda: V.bn_stats(out=bst[0:P, j4 * 6:(j4 + 1) * 6], in_=xb.sl(P, j4 * 512, (j4 + 1) * 512)), cover, [bst])
                      k.op("dve", lambda: V.bn_aggr(out=ssq4[0:P, 0:2], in_=bst[0:P, :]), [bst], [ssq4])
                      k.op("dve", lambda: V.scalar_tensor_tensor(out=ssq[0:P, 0:1], in0=ssq4[0:P, 0:1], scalar=ssq4[0:P, 0:1], in1=ssq4[0:P, 1:2],
                                                                 op0=ALU.mult, op1=ALU.add), [ssq4], [ssq])
                      rstd_from_ss(P, ssq[0:P, 0:1], ssq, 1.0, sm1[1], sm1[1][0:P, 0:1], sm1[2], sm1[2][0:P, 0:1])
                      for ap_, off_ in xb.parts:
                          k.op("dve", lambda: V.tensor_scalar(out=ap_[0:P, :], in0=ap_[0:P, :], scalar1=sm1[1][0:P, 0:1], scalar2=None,
                                                              op0=ALU.mult), cover + [sm1[1]], cover)
                      if tt < 8:
                          for g4 in range(4):
                              for j in range(4):
                                  kc = g4 * 4 + j
                                  k.mm(pt32, pt32[:, j * 128:(j + 1) * 128], xb.tls[0], xb.sl(128, kc * 128, (kc + 1) * 128), cst, ident,
                                       start=True, stop=True, last=(j == 3), transpose=True)
                              for j in range(4):
                                  kc = g4 * 4 + j
                                  e = k.ve()
                                  if e == "dve":
                                      k.op("dve", lambda: V.tensor_scalar(out=hT[:, kc, tok0:tok0 + 128], in0=pt32[:, j * 128:(j + 1) * 128],
                                                                          scalar1=gsT[:, kc:kc + 1], scalar2=shP[:, kc:kc + 1],
                                                                          op0=ALU.mult, op1=ALU.add), [pt32, gsT, shP], [hT])
                                  else:
                                      k.op("act", lambda: A.activation(out=hT[:, kc, tok0:tok0 + 128], in_=pt32[:, j * 128:(j + 1) * 128],
                                                                       func=AF.Identity, scale=gsT[:, kc:kc + 1], bias=shP[:, kc:kc + 1]),
                                           [pt32, gsT, shP], [hT])
                      else:
                          for kc in range(KC):
                              k.mm(pt32, pt32[:, kc * NS:(kc + 1) * NS], xb.tls[0], xb.sl(NS, kc * 128, (kc + 1) * 128), cst, ident[0:NS, 0:NS],
                                   start=True, stop=True, last=(kc == KC - 1), transpose=True)
                          t_ = wk[2]
                          k.op("dve", lambda: V.tensor_tensor(out=t_[:, 0:256].rearrange("p (a b) -> p a b", a=KC),
                                                              in0=pt32[:, 0:256].rearrange("p (a b) -> p a b", a=KC), in1=sc1T[:, :, :], op=ALU.mult),
                               [pt32, sc1T], [t_])
                          k.op("dve", lambda: V.tensor_tensor(out=hT[:, :, 1024:1040], in0=t_[:, 0:256].rearrange("p (a b) -> p a b", a=KC),
                                                              in1=shS[:, :, :], op=ALU.add), [t_, shS], [hT])

                  stage(2)
                  gp = psml
                  for (tt, P, tok0) in tiles:
                      if tt == 8:
                          continue
                      for kc in range(KC):
                          k.mm(gp, gp[:, tt * 8:(tt + 1) * 8], hT, hT[:, kc, tok0:tok0 + 128], wg, wg[:, l, kc, :],
                               start=(kc == 0), stop=(kc == KC - 1), last=(kc == KC - 1 and tt == 7))
                  gpv = gp[:, 0:64].rearrange("p (c e) -> p c e", c=8)
                  bi_b = big_bc[:, l, 0:4].unsqueeze(1).to_broadcast([128, 8, 4])
                  bf_b = big_bc[:, l, 4:8].unsqueeze(1).to_broadcast([128, 8, 4])
                  G_ = gt
                  k.op("dve", lambda: V.tensor_tensor(out=G_["IG"][:, :, :], in0=gpv[:, :, 0:4], in1=bi_b, op=ALU.add), [gp, big_bc], [G_["IG"]])
                  k.op("dve", lambda: V.tensor_tensor(out=G_["XF"][:, :, :], in0=gpv[:, :, 4:8], in1=bf_b, op=ALU.add), [gp, big_bc], [G_["XF"]])
                  k.op("act", lambda: A.activation(out=G_["XF"][:, :, :], in_=G_["XF"][:, :, :], func=AF.Exp, scale=-1.0), [G_["XF"]], [G_["XF"]])
                  k.op("act", lambda: A.activation(out=G_["LL"][:, :, :], in_=G_["XF"][:, :, :], func=AF.Ln, bias=1.0), [G_["XF"]], [G_["LL"]])
                  k.mm(pbc, pbc[:, 0:32], cst, tri, G_["LL"], G_["LL"][:, :, :].rearrange("p c h -> p (c h)"), start=True, stop=True)
                  k.op("dve", lambda: V.tensor_copy(out=G_["BL"][:, :, :].rearrange("p c h -> p (c h)"), in_=pbc[:, 0:32]), [pbc], [G_["BL"]])
                  k.op("dve", lambda: V.tensor_tensor(out=G_["U"][:, :, :], in0=G_["IG"][:, :, :], in1=G_["BL"][:, :, :], op=ALU.add),
                       [G_["IG"], G_["BL"]], [G_["U"]])
                  k.mm(pt32, pt32[0:32, 0:128], G_["U"], G_["U"][:, :, :].rearrange("p c h -> p (c h)"), cst, ident, start=True, stop=True, transpose=True)
                  k.op("dve", lambda: V.tensor_copy(out=UT[0:32, 0:128], in_=pt32[0:32, 0:128]), [pt32], [UT])
                  k.op("dve", lambda: V.tensor_tensor_scan(out=CUT[0:32, 0:128], data0=UT[0:32, 0:128], data1=UT[0:32, 0:128], initial=-1e30, op0=ALU.max, op1=ALU.max),
                       [UT], [CUT])
                  k.mm(pt32, pt32[:, 128:160], CUT, CUT[0:32, 0:128], cst, ident[0:32, 0:32], start=True, stop=True, transpose=True)
                  k.op("dve", lambda: V.tensor_copy(out=G_["CU"][:, :, :].rearrange("p c h -> p (c h)"), in_=pt32[:, 128:160]), [pt32], [G_["CU"]])
                  k.mm(pbc, pbc[:, 32:64], cst, e127, G_["CU"], G_["CU"][:, :, :].rearrange("p c h -> p (c h)"), start=True, stop=True)
                  k.op("dve", lambda: V.tensor_copy(out=G_["CUl"][:, :, :].rearrange("p c h -> p (c h)"), in_=pbc[:, 32:64]), [pbc], [G_["CUl"]])
                  k.mm(pbc, pbc[:, 64:96], cst, e127, G_["BL"], G_["BL"][:, :, :].rearrange("p c h -> p (c h)"), start=True, stop=True)
                  k.op("dve", lambda: V.tensor_copy(out=G_["BLl"][:, :, :].rearrange("p c h -> p (c h)"), in_=pbc[:, 64:96]), [pbc], [G_["BLl"]])
                  def mchain(mb, Ml):
                      for c in range(8):
                          k.op("dve", lambda: V.tensor_tensor(out=Ml[:, c, :], in0=mb[:, c, :], in1=G_["CUl"][:, c, :], op=ALU.max),
                               [mb, G_["CUl"]], [Ml])
                          k.op("dve", lambda: V.tensor_tensor(out=mb[:, c + 1, :], in0=Ml[:, c, :], in1=G_["BLl"][:, c, :], op=ALU.subtract),
                               [Ml, G_["BLl"]], [mb])

                  def chainB():
                      if COLL:
                          k.dma("sp", min_bc[:, :], coutM[l][0].partition_broadcast(128), [coutM_t[l]], [min_bc])
                          k.op("dve", lambda: V.tensor_scalar(out=mbc[:, 0, :], in0=min_bc[:, :], scalar1=msk_bc[:, 0:1], scalar2=None, op0=ALU.mult),
                               [min_bc, msk_bc], [mbc])
                      else:
                          k.op("dve", lambda: V.tensor_copy(out=mbc[:, 0, :], in_=mbc[:, 8, :]), [mbc], [mbc])
                      mchain(mbc, G_["Ml"])
                      mprev = mbc[:, 0:8, :]
                      k.op("dve", lambda: V.tensor_tensor(out=G_["M"][:, :, :], in0=G_["CU"][:, :, :], in1=mprev, op=ALU.max), [G_["CU"], mbc], [G_["M"]])
                      k.op("dve", lambda: V.tensor_scalar(out=G_["NEGM"][:, :, :], in0=G_["M"][:, :, :], scalar1=-1.0, scalar2=None, op0=ALU.mult), [G_["M"]], [G_["NEGM"]])
                      k.op("dve", lambda: V.tensor_tensor(out=G_["T1"][:, :, :], in0=mprev, in1=G_["M"][:, :, :], op=ALU.subtract), [mbc, G_["M"]], [G_["T1"]])
                      k.op("act", lambda: A.activation(out=G_["WI"][:, :, :], in_=G_["T1"][:, :, :], func=AF.Exp), [G_["T1"]], [G_["WI"]])
                      k.op("dve", lambda: V.tensor_tensor(out=G_["T1"][:, :, :], in0=G_["BL"][:, :, :], in1=G_["M"][:, :, :], op=ALU.subtract),
                           [G_["BL"], G_["M"]], [G_["T1"]])
                      k.op("act", lambda: A.activation(out=G_["ENM"][:, :, :], in_=G_["T1"][:, :, :], func=AF.Exp), [G_["T1"]], [G_["ENM"]])
                      k.op("dve", lambda: V.tensor_tensor(out=G_["T1"][:, :, :], in0=G_["U"][:, :, :], in1=G_["Ml"][:, :, :], op=ALU.subtract),
                           [G_["U"], G_["Ml"]], [G_["T1"]])
                      k.op("act", lambda: A.activation(out=G_["G"][:, :, :], in_=G_["T1"][:, :, :], func=AF.Exp), [G_["T1"]], [G_["G"]])
                      k.op("dve", lambda: V.tensor_tensor(out=G_["T1"][:, :, :], in0=mprev, in1=G_["Ml"][:, :, :], op=ALU.subtract), [mbc, G_["Ml"]], [G_["T1"]])
                      k.op("act", lambda: A.activation(out=G_["DEC"][:, :, :], in_=G_["T1"][:, :, :], func=AF.Exp), [G_["T1"]], [G_["DEC"]])
                      if half == NHALF - 1:
                          k.dma("sp", mpo[l:l + 1, :], mbc[0:1, 8, :], [mbc], [], pwrites=[out_t["mpo"]])

                  if COLL:
                      k.op("dve", lambda: V.memset(mbcA[:, 0, :], 0.0), [], [mbcA])
                      mchain(mbcA, G_["MlA"])
                      k.op("dve", lambda: V.tensor_tensor(out=G_["T1"][:, :, :], in0=G_["U"][:, :, :], in1=G_["MlA"][:, :, :], op=ALU.subtract),
                           [G_["U"], G_["MlA"]], [G_["T1"]])
                      k.op("act", lambda: A.activation(out=G_["GA"][:, :, :], in_=G_["T1"][:, :, :], func=AF.Exp), [G_["T1"]], [G_["GA"]])
                      k.op("dve", lambda: V.tensor_tensor(out=G_["T1"][:, :, :], in0=mbcA[:, 0:8, :], in1=G_["MlA"][:, :, :], op=ALU.subtract),
                           [mbcA, G_["MlA"]], [G_["T1"]])
                      k.op("act", lambda: A.activation(out=G_["DECA"][:, :, :], in_=G_["T1"][:, :, :], func=AF.Exp), [G_["T1"]], [G_["DECA"]])
                      k.dma("sp", cinM[l], mbcA[0:1, 8, :], [mbcA], [cinM_t[l]])
                      k.cc(cinM[l], coutM[l], [cinM_t[l]], [coutM_t[l]])
                  else:
                      chainB()
                  stage(3)
                  if half == 0:
                      for kc in range(KC):
                          k.mm(gp, gp[:, 64:72], hT, hT[:, kc, 1024:1152], wg, wg[:, l, kc, :], start=(kc == 0), stop=(kc == KC - 1))
                      stage(3.1)
                      S_ = sg
                      k.dma("sp", S_["m0"][:, :], smm[l], [], [S_["m0"]])
                      stage(3.2)
                      k.op("dve", lambda: V.tensor_tensor(out=S_["ig"][:, :], in0=gp[0:NS, 64:68], in1=big_bc[0:NS, l, 0:4], op=ALU.add), [gp, big_bc], [S_["ig"]])
                      k.op("dve", lambda: V.tensor_tensor(out=S_["xf"][:, :], in0=gp[0:NS, 68:72], in1=big_bc[0:NS, l, 4:8], op=ALU.add), [gp, big_bc], [S_["xf"]])
                      k.op("act", lambda: A.activation(out=S_["xf"][:, :], in_=S_["xf"][:, :], func=AF.Exp, scale=-1.0), [S_["xf"]], [S_["xf"]])
                      k.op("act", lambda: A.activation(out=S_["ll"][:, :], in_=S_["xf"][:, :], func=AF.Ln, bias=1.0), [S_["xf"]], [S_["ll"]])
                      k.op("dve", lambda: V.tensor_tensor(out=S_["a"][:, :], in0=S_["m0"][:, :], in1=S_["ll"][:, :], op=ALU.subtract), [S_["m0"], S_["ll"]], [S_["a"]])
                      k.op("dve", lambda: V.tensor_tensor(out=S_["mt"][:, :], in0=S_["a"][:, :], in1=S_["ig"][:, :], op=ALU.max), [S_["a"], S_["ig"]], [S_["mt"]])
                      k.op("dve", lambda: V.tensor_tensor(out=S_["t"][:, :], in0=S_["ig"][:, :], in1=S_["mt"][:, :], op=ALU.subtract), [S_["ig"], S_["mt"]], [S_["t"]])
                      k.op("act", lambda: A.activation(out=S_["wa"][:, :], in_=S_["t"][:, :], func=AF.Exp), [S_["t"]], [S_["wa"]])
                      k.op("dve", lambda: V.tensor_tensor(out=S_["t"][:, :], in0=S_["a"][:, :], in1=S_["mt"][:, :], op=ALU.subtract), [S_["a"], S_["mt"]], [S_["t"]])
                      k.op("act", lambda: A.activation(out=S_["wi"][:, :], in_=S_["t"][:, :], func=AF.Exp), [S_["t"]], [S_["wi"]])
                      k.op("act", lambda: A.activation(out=S_["enm"][:, :], in_=S_["mt"][:, :], func=AF.Exp, scale=-1.0), [S_["mt"]], [S_["enm"]])
                      stage(3.3)
                      k.dma("sp", mso[l], S_["mt"][:, :], [S_["mt"]], [], pwrites=[out_t["mso"]])

                  stage(4)
                  for i in range(NHD):
                      unit, uj = wacquire(("inp", l, 4 * i + 0))
                      pji = 0
                      for j in range(4):
                          for nbk in range(2):
                              pjt = pj[pji % 2]
                              pji += 1
                              for kc in range(KC):
                                  k.mm(pjt, pjt[:, :], unit, unit[:, kc, j * 128:(j + 1) * 128], hT, hT[:, kc, nbk * 512:(nbk + 1) * 512],
                                       start=(kc == 0), stop=(kc == KC - 1))
                              if j < 2:
                                  e = k.ve()
                                  if e == "dve":
                                      k.op("dve", lambda: V.tensor_copy(out=qT[:, j, nbk * 512:(nbk + 1) * 512], in_=pjt[:, :]), [pjt], [qT])
                                  else:
                                      k.op("act", lambda: A.copy(out=qT[:, j, nbk * 512:(nbk + 1) * 512], in_=pjt[:, :]), [pjt], [qT])
                              else:
                                  k.op("act", lambda: A.mul(out=kT[:, j - 2, nbk * 512:(nbk + 1) * 512], in_=pjt[:, :], mul=0.0625), [pjt], [kT])
                      if half == 0:
                          pjt = pj[pji % 2]
                          pji += 1
                          proj_tok(unit, 8, NS, 1024, pjt)
                          k.op("dve", lambda: V.tensor_copy(out=qk_s[:, 0:256], in_=pjt[0:NS, 0:256]), [pjt], [qk_s])
                          k.op("act", lambda: A.mul(out=qk_s[:, 256:512], in_=pjt[0:NS, 256:512], mul=0.0625), [pjt], [qk_s])
                      wrelease(uj)
                      stage(5)
                      for g2 in range(2):
                          for c4 in range(4):
                              c = g2 * 4 + c4
                              for dt_ in range(2):
                                  k.mm(ptb, ptb[:, (c4 * 2 + dt_) * 128:(c4 * 2 + dt_ + 1) * 128], kT, kT[:, dt_, c * 128:(c + 1) * 128], identb, identb[:, :],
                                       start=True, stop=True, last=(c4 == 3 and dt_ == 1), transpose=True)
                          k.op(k.ve(), (lambda: V.tensor_copy(out=ktok[:, g2 * 4:(g2 + 1) * 4, :].rearrange("p a b -> p (a b)"), in_=ptb[:, :])) if k.flip
                               else (lambda: A.copy(out=ktok[:, g2 * 4:(g2 + 1) * 4, :].rearrange("p a b -> p (a b)"), in_=ptb[:, :])), [ptb], [ktok])
                      unit, uj = wacquire(("inp", l, 4 * i + 1))
                      for (tt, P, tok0) in tiles:
                          pjt = pj[pji % 2]
                          pji += 1
                          proj_tok(unit, tt, P, tok0, pjt)
                          if tt < 8:
                              k.op("dve", lambda: V.tensor_copy(out=vtok[:, tt, :], in_=pjt[:, 0:256]), [pjt], [vtok])
                          else:
                              k.op("dve", lambda: V.tensor_copy(out=v_s[:, :], in_=pjt[0:NS, 0:256]), [pjt], [v_s])
                          k.op("act", lambda: A.activation(out=og[tt][0:P, :], in_=pjt[0:P, 256:512], func=AF.Sigmoid), [pjt], [og[tt]])
                      wrelease(uj)

                      stage(6)
                      def passB():
                          k.op("act", lambda: A.copy(out=CTb[:, :, :], in_=CT32[i][:, :, :]), [CT32[i]], [CTb])
                          k.op("dve", lambda: V.tensor_copy(out=nb[:, :], in_=n32[i][:, :]), [n32[i]], [nb])
                          def headA(c):
                              cs = slice(c * 128, (c + 1) * 128)
                              Dg, Wt, Pt, kg = wk[0], wk[1], wkb[0], wkb[1]
                              k.op("act", lambda: A.activation(out=Dg[:, 0:128], in_=ident, func=AF.Identity, scale=G_["NEGM"][:, c, i:i + 1]), [cst, G_["NEGM"]], [Dg])
                              k.op("act", lambda: A.activation(out=kg[:, 0:256], in_=ktok[:, c, :], func=AF.Identity, scale=G_["G"][:, c, i:i + 1]),
                                   [ktok, G_["G"]], [kg])
                              k.mm(pbc, pbc[:, :], cst, ones, Dg, Dg[:, 0:128], start=True, stop=False)
                              k.mm(pbc, pbc[:, :], cst, ident, cst, maskneg, start=False, stop=True)
                              for dt_ in range(2):
                                  k.mm(pS, pS[:, :], kT, kT[:, dt_, cs], qT, qT[:, dt_, cs], start=(dt_ == 0), stop=(dt_ == 1))
                              k.op("act", lambda: A.activation(out=Wt[:, 0:128], in_=pbc[:, :], func=AF.Exp, bias=G_["U"][:, c, i:i + 1]), [pbc, G_["U"]], [Wt])

                          def headB(c):
                              cs = slice(c * 128, (c + 1) * 128)
                              Dg, Wt, Pt, kg = wk[0], wk[1], wkb[0], wkb[1]
                              k.op("dve", lambda: V.tensor_tensor(out=Pt[:, 0:128], in0=pS[:, :], in1=Wt[:, 0:128], op=ALU.mult), [pS, Wt], [Pt])
                              for dt_ in range(2):
                                  k.mm(pinter, pinter[:, :], qT, qT[:, dt_, cs], CTb, CTb[:, dt_, :], start=(dt_ == 0), stop=(dt_ == 1))
                              for dt_ in range(2):
                                  k.mm(psml, psml[:, 80:81], qT, qT[:, dt_, cs], nb, nb[:, dt_:dt_ + 1], start=(dt_ == 0), stop=(dt_ == 1))
                              k.mm(pintra, pintra[:, :], Pt, Pt[:, 0:128], vtok, vtok[:, c, :], start=True, stop=True)
                              k.mm(psml, psml[:, 82:83], Pt, Pt[:, 0:128], onesb, onesb[:, :], start=True, stop=True)
                              for dt_ in range(2):
                                  k.mm(pupd, pupd[:, dt_, :], kg, kg[:, dt_ * 128:(dt_ + 1) * 128], vtok, vtok[:, c, :], start=True, stop=True, last=(dt_ == 1))
                              for dt_ in range(2):
                                  k.mm(psml, psml[:, 84 + dt_:85 + dt_], kg, kg[:, dt_ * 128:(dt_ + 1) * 128], onesb, onesb[:, :], start=True, stop=True, last=(dt_ == 1))

                          def bufs(c):
                              if c % 2 == 0:
                                  return wk[3], sm1[3], sm1[4], sm1[5]
                              return wk[5], sm1[0], sm1[1], sm1[2]

                          def tail1(c):
                              num, d1, d2, d3 = bufs(c)
                              intra_sb = wk[2]
                              k.op("act", lambda: A.copy(out=intra_sb[:, 0:256], in_=pintra[:, :]), [pintra], [intra_sb])
                              k.op("dve", lambda: V.scalar_tensor_tensor(out=num[:, 0:256], in0=pinter[:, :], scalar=G_["WI"][:, c, i:i + 1], in1=intra_sb[:, 0:256],
                                                                         op0=ALU.mult, op1=ALU.add), [pinter, G_["WI"], intra_sb], [num])
                              k.op("dve", lambda: V.tensor_copy(out=d1[:, 0:1], in_=psml[:, 82:83]), [psml], [d1])
                              k.op("dve", lambda: V.scalar_tensor_tensor(out=d2[:, 0:1], in0=psml[:, 80:81], scalar=G_["WI"][:, c, i:i + 1], in1=d1[:, 0:1],
                                                                         op0=ALU.mult, op1=ALU.add), [psml, G_["WI"], d1], [d2])
                              k.op("dve", lambda: V.scalar_tensor_tensor(out=CT32[i][:, :, :].rearrange("p a b -> p (a b)"),
                                                                         in0=CT32[i][:, :, :].rearrange("p a b -> p (a b)"), scalar=G_["DEC"][:, c, i:i + 1],
                                                                         in1=pupd[:, :, :].rearrange("p a b -> p (a b)"), op0=ALU.mult, op1=ALU.add),
                                   [CT32[i], G_["DEC"], pupd], [CT32[i]])
                              k.op("dve", lambda: V.scalar_tensor_tensor(out=n32[i][:, :], in0=n32[i][:, :], scalar=G_["DEC"][:, c, i:i + 1], in1=psml[:, 84:86],
                                                                         op0=ALU.mult, op1=ALU.add), [n32[i], G_["DEC"], psml], [n32[i]])
                              k.op("act", lambda: A.copy(out=CTb[:, :, :], in_=CT32[i][:, :, :]), [CT32[i]], [CTb])
                              k.op("dve", lambda: V.tensor_copy(out=nb[:, :], in_=n32[i][:, :]), [n32[i]], [nb])

                          def tail2(c):
                              num, d1, d2, d3 = bufs(c)
                              k.op("dve", lambda: V.tensor_scalar(out=d2[:, 1:2], in0=d2[:, 0:1], scalar1=-1.0, scalar2=None, op0=ALU.mult), [d2], [d2])
                              k.op("dve", lambda: V.tensor_tensor(out=d2[:, 0:1], in0=d2[:, 0:1], in1=d2[:, 1:2], op=ALU.max), [d2], [d2])
                              k.op("dve", lambda: V.tensor_tensor(out=d2[:, 0:1], in0=d2[:, 0:1], in1=G_["ENM"][:, c, i:i + 1], op=ALU.max), [d2, G_["ENM"]], [d2])
                              k.op("dve", lambda: V.reciprocal(out=d3[:, 0:1], in_=d2[:, 0:1]), [d2], [d3])
                              k.op("dve", lambda: V.scalar_tensor_tensor(out=og[c][:, :], in0=num[:, 0:256], scalar=d3[:, 0:1], in1=og[c][:, :],
                                                                         op0=ALU.mult, op1=ALU.mult), [num, d3, og[c]], [og[c]])
                              sq = wk[4]
                              k.op("act", lambda: A.activation(out=sq[:, 0:256], in_=og[c][:, :], func=AF.Square), [og[c]], [sq])
                              k.op("dve", lambda: V.tensor_reduce(out=ssc[:, c:c + 1], in_=sq[:, 0:256], axis=AX.X, op=ALU.add), [sq], [ssc])

                          headA(0)
                          headB(0)
                          tail1(0)
                          yield "pro"
                          for c in range(8):
                              if c < 7:
                                  headA(c + 1)
                              tail2(c)
                              yield "a"
                              if c < 7:
                                  headB(c + 1)
                                  yield "b"
                                  tail1(c + 1)
                              yield "c"
                          rstd_from_ss(128, ssc[:, 0:8], ssc, float(HD), rr, rr[:, 0:8], ssc2, ssc2[:, 0:8])
                          if half == NHALF - 1:
                              k.dma("sp", Cp[l, i], CT32[i][:, :, :], [CT32[i]], [], pwrites=[out_t["Cp"]])
                              k.dma("sp", npo[l, i], n32[i][:, :], [n32[i]], [], pwrites=[out_t["npo"]])


                      def sampleM():
                          if half == 0:
                              S_ = sg
                              k.dma("sp", n_s[:, :], sn[l, :, i * HD:(i + 1) * HD], [], [n_s])
                              nq, qk, s_, den, rec = ssm[0], ssm[1], ssm[2], ssm[3], ssm[4]
                              jk = wk[5]
                              k.op("dve", lambda: V.tensor_tensor(out=jk[0:NS, 0:256], in0=n_s[:, :], in1=qk_s[:, 0:256], op=ALU.mult), [n_s, qk_s], [jk])
                              k.op("dve", lambda: V.tensor_reduce(out=nq[:, 0:1], in_=jk[0:NS, 0:256], axis=AX.X, op=ALU.add), [jk], [nq])
                              k.op("dve", lambda: V.tensor_tensor(out=jk[0:NS, 0:256], in0=qk_s[:, 256:512], in1=qk_s[:, 0:256], op=ALU.mult), [qk_s], [jk])
                              k.op("dve", lambda: V.tensor_reduce(out=qk[:, 0:1], in_=jk[0:NS, 0:256], axis=AX.X, op=ALU.add), [jk], [qk])
                              k.op("dve", lambda: V.tensor_tensor(out=s_[:, 0:1], in0=qk[:, 0:1], in1=S_["wa"][:, i:i + 1], op=ALU.mult), [qk, S_["wa"]], [s_])
                              k.op("dve", lambda: V.tensor_scalar(out=dgw[:, 0, :], in0=ident[0:NS, 0:NS], scalar1=S_["wi"][:, i:i + 1], scalar2=None, op0=ALU.mult),
                                   [cst, S_["wi"]], [dgw])
                              k.op("dve", lambda: V.tensor_scalar(out=dgw[:, 1, :], in0=ident[0:NS, 0:NS], scalar1=S_["wa"][:, i:i + 1], scalar2=None, op0=ALU.mult),
                                   [cst, S_["wa"]], [dgw])
                              k.mm(pbc, pbc[:, 0:32], cst, ones[0:NS, :], dgw, dgw[:, :, :].rearrange("p a b -> p (a b)"), start=True, stop=True)
                              k.op("dve", lambda: V.tensor_copy(out=wbc[:, :, :].rearrange("p a b -> p (a b)"), in_=pbc[:, 0:32]), [pbc], [wbc])
                              for r in range(2):
                                  k.mm(pt32, pt32[:, r * NS:(r + 1) * NS], v_s, v_s[:, r:256:2], cst, ident[0:NS, 0:NS], start=True, stop=True,
                                       last=(r == 1), transpose=True)
                              k.op("dve", lambda: V.tensor_tensor(out=vsT[:, :, :], in0=pt32[:, 0:32].rearrange("p (a b) -> p a b", a=2),
                                                                  in1=wbc[:, 1:2, :].to_broadcast([128, 2, NS]), op=ALU.mult), [pt32, wbc], [vsT])
                              qks_t = Tl(None, "qks")
                              k.dma("sp", qks[l, i], qk_s[:, :], [qk_s], [], pwrites=[qks_t])

                              bcs = [gate_bc, gate_s]

                              def sloads(j):
                                  k.dma("sp", Cbuf[j % 2][:, :, :], sC[l, j, i], [], [Cbuf[j % 2]])
                                  k.dma("sp", bcs[j % 2][:, :], qks[l, i, j].partition_broadcast(128), [qks_t], [bcs[j % 2]])

                              sloads(0)
                              yield "pre"
                              for j in range(NS):
                                  cb = Cbuf[j % 2]
                                  bc = bcs[j % 2]
                                  if j + 1 < NS:
                                      sloads(j + 1)
                                  k.op("dve", lambda: V.tensor_tensor(out=psb[:, :].rearrange("p (a b) -> p a b", a=2), in0=cb[:, :, :],
                                                                      in1=bc[:, 0:256].unsqueeze(1).to_broadcast([128, 2, 256]), op=ALU.mult), [cb, bc], [psb])
                                  k.op("dve", lambda: V.tensor_reduce(out=cqT[:, :, j], in_=psb[:, :].rearrange("p (a b) -> p a b", a=2), axis=AX.X, op=ALU.add),
                                       [psb], [cqT])
                                  k.op("act", lambda: A.activation(out=cb[:, :, :], in_=cb[:, :, :], func=AF.Identity, scale=wbc[:, 0, j:j + 1]), [cb, wbc], [cb])
                                  for r in range(2):
                                      k.op("dve", lambda: V.scalar_tensor_tensor(out=cb[:, r, :], in0=bc[:, 256:512], scalar=vsT[:, r, j:j + 1], in1=cb[:, r, :],
                                                                                 op0=ALU.mult, op1=ALU.add), [bc, vsT, cb], [cb])
                                  k.dma("sp", Cs[l, j, i], cb[:, :, :], [cb], [], pwrites=[out_t["Cs"]])
                                  yield "j"
                              for r in range(2):
                                  k.mm(pt32, pt32[0:NS, 128 * r:128 * (r + 1)], cqT, cqT[:, r, :], cst, ident, start=True, stop=True, last=(r == 1), transpose=True)
                              k.op("dve", lambda: V.tensor_copy(out=cq_tok[:, :].rearrange("p (a r) -> p r a", r=2),
                                                                in_=pt32[0:NS, 0:256].rearrange("p (r a) -> p r a", r=2)), [pt32], [cq_tok])
                              tmpv = wk[2]
                              k.op("dve", lambda: V.tensor_scalar(out=tmpv[0:NS, 0:256], in0=v_s[:, :], scalar1=s_[:, 0:1], scalar2=None, op0=ALU.mult), [v_s, s_], [tmpv])
                              num = wk[3]
                              k.op("dve", lambda: V.scalar_tensor_tensor(out=num[0:NS, 0:256], in0=cq_tok[:, :], scalar=S_["wi"][:, i:i + 1], in1=tmpv[0:NS, 0:256],
                                                                         op0=ALU.mult, op1=ALU.add), [cq_tok, S_["wi"], tmpv], [num])
                              k.op("dve", lambda: V.scalar_tensor_tensor(out=den[:, 0:1], in0=nq[:, 0:1], scalar=S_["wi"][:, i:i + 1], in1=s_[:, 0:1],
                                                                         op0=ALU.mult, op1=ALU.add), [nq, S_["wi"], s_], [den])
                              k.op("dve", lambda: V.tensor_scalar(out=den[:, 1:2], in0=den[:, 0:1], scalar1=-1.0, scalar2=None, op0=ALU.mult), [den], [den])
                              k.op("dve", lambda: V.tensor_tensor(out=den[:, 0:1], in0=den[:, 0:1], in1=den[:, 1:2], op=ALU.max), [den], [den])
                              k.op("dve", lambda: V.tensor_tensor(out=den[:, 0:1], in0=den[:, 0:1], in1=S_["enm"][:, i:i + 1], op=ALU.max), [den, S_["enm"]], [den])
                              k.op("dve", lambda: V.reciprocal(out=rec[:, 0:1], in_=den[:, 0:1]), [den], [rec])
                              k.op("dve", lambda: V.scalar_tensor_tensor(out=og[8][0:NS, :], in0=num[0:NS, 0:256], scalar=rec[:, 0:1], in1=og[8][0:NS, :],
                                                                         op0=ALU.mult, op1=ALU.mult), [num, rec, og[8]], [og[8]])
                              sq = wk[4]
                              d1 = sm1[3]
                              k.op("act", lambda: A.activation(out=sq[0:NS, 0:256], in_=og[8][0:NS, :], func=AF.Square), [og[8]], [sq])
                              k.op("dve", lambda: V.tensor_reduce(out=d1[0:NS, 1:2], in_=sq[0:NS, 0:256], axis=AX.X, op=ALU.add), [sq], [d1])
                              rstd_from_ss(NS, d1[0:NS, 1:2], d1, float(HD), rr, rr[0:NS, 8:9], d1, d1[0:NS, 2:3])
                              k.op("dve", lambda: V.tensor_scalar(out=tmpv[0:NS, 0:256], in0=qk_s[:, 256:512], scalar1=S_["wa"][:, i:i + 1], scalar2=None, op0=ALU.mult),
                                   [qk_s, S_["wa"]], [tmpv])
                              k.op("dve", lambda: V.scalar_tensor_tensor(out=n_s[:, :], in0=n_s[:, :], scalar=S_["wi"][:, i:i + 1], in1=tmpv[0:NS, 0:256],
                                                                         op0=ALU.mult, op1=ALU.add), [n_s, S_["wi"], tmpv], [n_s])
                              k.dma("sp", nso[l, :, i * HD:(i + 1) * HD], n_s[:, :], [n_s], [], pwrites=[out_t["nso"]])


                      def passA():
                          kg = wkb[1]
                          for c in range(8):
                              k.op("dve", lambda: V.tensor_scalar(out=kg[:, 0:256], in0=ktok[:, c, :], scalar1=G_["GA"][:, c, i:i + 1], scalar2=None, op0=ALU.mult),
                                   [ktok, G_["GA"]], [kg])
                              for dt_ in range(2):
                                  k.mm(pupd, pupd[:, dt_, :], kg, kg[:, dt_ * 128:(dt_ + 1) * 128], vtok, vtok[:, c, :], start=True, stop=True, last=(dt_ == 1))
                              for dt_ in range(2):
                                  k.mm(psml, psml[:, 84 + dt_:85 + dt_], kg, kg[:, dt_ * 128:(dt_ + 1) * 128], onesb, onesb[:, :], start=True, stop=True, last=(dt_ == 1))
                              k.op("dve", lambda: V.scalar_tensor_tensor(out=CT32[i][:, :, :].rearrange("p a b -> p (a b)"),
                                                                         in0=CT32[i][:, :, :].rearrange("p a b -> p (a b)"), scalar=G_["DECA"][:, c, i:i + 1],
                                                                         in1=pupd[:, :, :].rearrange("p a b -> p (a b)"), op0=ALU.mult, op1=ALU.add),
                                   [CT32[i], G_["DECA"], pupd], [CT32[i]])
                              k.op("dve", lambda: V.scalar_tensor_tensor(out=n32[i][:, :], in0=n32[i][:, :], scalar=G_["DECA"][:, c, i:i + 1], in1=psml[:, 84:86],
                                                                         op0=ALU.mult, op1=ALU.add), [n32[i], G_["DECA"], psml], [n32[i]])
                          k.dma("sp", cinC[l][i][:, 0:512], CT32[i][:, :, :].rearrange("p a b -> p (a b)"), [CT32[i]], [], pwrites=[cinC_t[l][i]])
                          k.dma("sp", cinC[l][i][:, 512:514], n32[i][:, :], [n32[i]], [], pwrites=[cinC_t[l][i]])
                          k.cc(cinC[l][i], coutC[l][i], [cinC_t[l][i]], [coutC_t[l][i]])
                      def loadState():
                          k.dma("sp", CT32[i][:, :, :].rearrange("p a b -> p (a b)"), coutC[l][i][0:128, 0:512], [coutC_t[l][i]], [CT32[i]])
                          k.dma("sp", n32[i][:, :], coutC[l][i][0:128, 512:514], [coutC_t[l][i]], [n32[i]])
                          k.op("dve", lambda: V.tensor_scalar(out=CT32[i][:, :, :], in0=CT32[i][:, :, :], scalar1=msk_bc[:, 0:1], scalar2=None, op0=ALU.mult),
                               [CT32[i], msk_bc], [CT32[i]])
                          k.op("dve", lambda: V.tensor_scalar(out=n32[i][:, :], in0=n32[i][:, :], scalar1=msk_bc[:, 0:1], scalar2=None, op0=ALU.mult),
                               [n32[i], msk_bc], [n32[i]])
                      def drain(g):
                          for _ in g:
                              pass

                      if COLL:
                          passA()
                          stage(7)
                          gS = sampleM()
                          next(gS)
                          for _ in range(4):
                              next(gS)
                          if i == 0:
                              chainB()
                          loadState()
                          gB = passB()
                          next(gB)
                          for c in range(8):
                              assert next(gB) == "a"
                              next(gS, None)
                              r_ = next(gB)
                              if r_ == "b":
                                  if c % 2 == 0:
                                      next(gS, None)
                                  assert next(gB) == "c"
                          drain(gB)
                          drain(gS)
                      else:
                          drain(passB())
                          stage(7)
                          if half == 0:
                              drain(sampleM())
                      stage(8)
                      k.dma("sp", gmh_bc[:, :], gmh[l, i * 256:(i + 1) * 256].partition_broadcast(128), [], [gmh_bc])
                      k.dma("sp", gcmv_bc[:, :], gcmv[l, i * 256:(i + 1) * 256].partition_broadcast(128), [], [gcmv_bc])
                      wst = wk[5]
                      k.dma("sp", wst[:, 0:128], wsT[l, i], [], [wst])
                      k.op("dve", lambda: V.tensor_tensor(out=wsTb[:, i, :], in0=wst[:, 0:128], in1=tri, op=ALU.mult), [wst, cst], [wsTb])
                      unitC, ujC = wacquire(("inp", l, 4 * i + 2))
                      unitD, ujD = wacquire(("inp", l, 4 * i + 3))
                      pairs = [(pj[0], pj[1]), (pupd512, psb)]

                      def projCD(ti):
                          tt_, P_, tok0_ = tiles[ti]
                          pC_, pD_ = pairs[ti % 2]
                          proj_tok(unitC, tt_, P_, tok0_, pC_)
                          proj_tok(unitD, tt_, P_, tok0_, pD_)

                      projCD(0)
                      for ti, (tt, P, tok0) in enumerate(tiles):
                          pC, pD = pairs[ti % 2]
                          if ti + 1 < len(tiles):
                              projCD(ti + 1)
                          sz, gv, vn, mg = wk[0], wk[1], wk[2], wkb[0]
                          gu, szc, hc = wk[4], wk[5], wkb[0]
                          k.op("act", lambda: A.activation(out=gv[0:P, 0:256], in_=pC[0:P, 256:512], func=AF.Gelu), [pC], [gv])
                          k.op("act", lambda: A.activation(out=gu[0:P, 0:256], in_=pD[0:P, 0:256], func=AF.Gelu), [pD], [gu])
                          k.op("act", lambda: A.activation(out=sz[0:P, 0:256], in_=pC[0:P, 0:256], func=AF.Silu), [pC], [sz])
                          k.op("act", lambda: A.activation(out=szc[0:P, 0:256], in_=pD[0:P, 256:512], func=AF.Silu), [pD], [szc])
                          st6, mv = sm1[6], sm1[7]
                          k.op("dve", lambda: V.bn_stats(out=st6[0:P, 0:6], in_=gv[0:P, 0:256]), [gv], [st6])
                          k.op("dve", lambda: V.bn_aggr(out=mv[0:P, 0:2], in_=st6[0:P, 0:6]), [st6], [mv])
                          k.op("dve", lambda: V.tensor_scalar(out=mv[0:P, 2:3], in0=mv[0:P, 1:2], scalar1=EPS, scalar2=None, op0=ALU.add), [mv], [mv])
                          k.op("act", lambda: A.activation(out=mv[0:P, 2:3], in_=mv[0:P, 2:3], func=AF.Sqrt), [mv], [mv])
                          k.op("dve", lambda: V.tensor_tensor(out=sz[0:P, 0:256], in0=sz[0:P, 0:256], in1=gmh_bc[0:P, :], op=ALU.mult), [sz, gmh_bc], [sz])
                          k.op("dve", lambda: V.scalar_tensor_tensor(out=mg[0:P, 0:256], in0=og[tt][0:P, :], scalar=rr[0:P, tt:tt + 1], in1=sz[0:P, 0:256],
                                                                     op0=ALU.mult, op1=ALU.mult), [og[tt], rr, sz], [mg])
                          for dt_ in range(2):
                              k.mm(ptb, ptb[:, dt_ * 128:dt_ * 128 + P], mg, mg[0:P, dt_ * 128:(dt_ + 1) * 128], identb, identb[0:P, 0:P],
                                   start=True, stop=True, last=(dt_ == 1), transpose=True)
                          k.op("dve", lambda: V.reciprocal(out=mv[0:P, 3:4], in_=mv[0:P, 2:3]), [mv], [mv])
                          k.op("dve", lambda: V.tensor_scalar(out=gv[0:P, 0:256], in0=gv[0:P, 0:256], scalar1=mv[0:P, 0:1], scalar2=mv[0:P, 3:4],
                                                              op0=ALU.subtract, op1=ALU.mult), [gv, mv], [gv])
                          smx = wk[3]
                          if tt < 8:
                              vnb = wkb[1]
                              k.op("dve", lambda: V.tensor_tensor(out=vnb[:, 0:256], in0=gv[:, 0:256], in1=gcmv_bc[:, :], op=ALU.mult), [gv, gcmv_bc], [vnb])
                              k.mm(pintra, pintra[:, :], wsTb, wsTb[:, i, :], vnb, vnb[:, 0:256], start=True, stop=True)
                          k.op("act", lambda: A.copy(out=mT[:, 2 * i:2 * i + 2, tok0:tok0 + P],
                                                     in_=ptb[:, 0:256].rearrange("p (a b) -> p a b", a=2)[:, :, 0:P]), [ptb], [mT])
                          if tt < 8:
                              k.op("act", lambda: A.activation(out=smx[:, 0:256], in_=pintra[:, :], func=AF.Identity, bias=bsT_sb[:, l, i:i + 1]),
                                   [pintra, bsT_sb], [smx])
                          else:
                              k.op("dve", lambda: V.tensor_tensor(out=vn[0:NS, 0:256], in0=gv[0:NS, 0:256], in1=gcmv_bc[0:NS, :], op=ALU.mult), [gv, gcmv_bc], [vn])
                              k.dma("sp", vro[l, :, i * 256:(i + 1) * 256], vn[0:NS, 0:256], [vn], [], pwrites=[out_t["vro"]])
                              k.op("dve", lambda: V.tensor_scalar(out=smx[0:NS, 0:256], in0=vn[0:NS, 0:256], scalar1=ws00_bc[0:NS, l, i:i + 1],
                                                                  scalar2=ws00_bc[0:NS, l, 4 + i:5 + i], op0=ALU.mult, op1=ALU.add), [vn, ws00_bc], [smx])
                          k.op("dve", lambda: V.tensor_tensor(out=gu[0:P, 0:256], in0=gu[0:P, 0:256], in1=szc[0:P, 0:256], op=ALU.mult), [gu, szc], [gu])
                          k.op("dve", lambda: V.tensor_tensor(out=hc[0:P, 0:256], in0=gu[0:P, 0:256], in1=smx[0:P, 0:256], op=ALU.mult), [gu, smx], [hc])
                          for dt_ in range(2):
                              k.mm(ptb, ptb[:, 512 + dt_ * 128:512 + dt_ * 128 + P], hc, hc[0:P, dt_ * 128:(dt_ + 1) * 128], identb, identb[0:P, 0:P],
                                   start=True, stop=True, last=(dt_ == 1), transpose=True)
                          k.op("act", lambda: A.copy(out=mT[:, 8 + 2 * i:8 + 2 * i + 2, tok0:tok0 + P],
                                                     in_=ptb[:, 512:768].rearrange("p (a b) -> p a b", a=2)[:, :, 0:P]), [ptb], [mT])
                      wrelease(ujC)
                      wrelease(ujD)
                      if EARLY_ADA and l == 0:
                          ada_unit(1, 2 * i, adaT2)
                          ada_unit(1, 2 * i + 1, adaT2)

                  stage(9)
                  for j in range(4):
                      unitG, ujG = wacquire(("ada", l, 8 + j))
                      k.dma("sp", gate_bc[:, :], bgate[l, j * 512:(j + 1) * 512].partition_broadcast(128), [], [gate_bc])
                      for kc in range(KC):
                          k.mm(pj[0], pj[0][:, :], scTb, scTb[:, kc, :], unitG, unitG[:, kc, :], start=(kc == 0), stop=(kc == KC - 1))
                      if half == 0:
                          for kc in range(KC):
                              k.mm(pj[1], pj[1][:, :], scTs, scTs[:, kc, :], unitG, unitG[:, kc, :], start=(kc == 0), stop=(kc == KC - 1))
                          k.op("dve", lambda: V.tensor_tensor(out=gate_s[0:NS, :], in0=pj[1][0:NS, :], in1=gate_bc[0:NS, :], op=ALU.add), [pj[1], gate_bc], [gate_s])
                      k.op("dve", lambda: V.tensor_tensor(out=gate_bc[:, :], in0=pj[0][:, :], in1=gate_bc[:, :], op=ALU.add), [pj[0], gate_bc], [gate_bc])
                      wrelease(ujG)
                      unitO, ujO = wacquire(("out", l, j))
                      for ti, (tt, P, tok0) in enumerate(tiles):
                          pjt = pj[ti % 2]
                          for kc in range(KC):
                              k.mm(pjt, pjt[:, :], mT, mT[:, kc, tok0:tok0 + 128], unitO, unitO[:, kc, :], start=(kc == 0), stop=(kc == KC - 1))
                          xo = wk[ti % 2]
                          tg = wk[2 + ti % 2]
                          src_t = xin_t(tt, half)
                          k.dma("sp", xo[0:P, :], xin(tt, half)[:, j * 512:(j + 1) * 512], [src_t] if src_t else [], [xo])
                          gsrc = gate_bc if tt < 8 else gate_s
                          k.op("dve", lambda: V.tensor_tensor(out=tg[0:P, :], in0=pjt[0:P, :], in1=gsrc[0:P, :], op=ALU.mult), [pjt, gsrc], [tg])
                          k.op("dve", lambda: V.tensor_tensor(out=tg[0:P, :], in0=tg[0:P, :], in1=xo[0:P, :], op=ALU.add), [tg, xo], [tg])
                          row0 = half * 1024 + tt * 128 if tt < 8 else NT
                          dt_t = xs_t[(xout_nm, half * 8 + tt if tt < 8 else NT // 128)]
                          k.dma("sp", xout[row0:row0 + P, j * 512:(j + 1) * 512], tg[0:P, :], [tg], [], pwrites=[dt_t])
                      wrelease(ujO)


        except StopBuild:
            pass
        gf = [wk[0], wk[1], wk[2], wk[3]]
        for j in range(4):
            k.dma("sp", gf[j][:, :], gfin[0, j * 512:(j + 1) * 512].partition_broadcast(128), [], [gf[j]])
        all_tiles = [(h_ * 8 + c, 128, h_ * 1024 + c * 128) for h_ in range(NHALF) for c in range(8)] + [(NT // 128, NS, NT)]
        def fload(ti):
            gt__, P_, row0_ = all_tiles[ti]
            xb_ = xbs[ti % 3]
            for ap_, off_ in xb_.parts:
                w_ = ap_.shape[1]
                k.dma("sp", ap_[0:P_, :], xs2[row0_:row0_ + P_, off_:off_ + w_], [xs_t[("xs2", gt__)]], xb_.tls)

        fload(0)
        fload(1)
        for ti, (gt_, P, row0) in enumerate(all_tiles):
            if ti + 2 < len(all_tiles):
                fload(ti + 2)
            xb = xbs[ti % 3]
            cover = xb.tls
            ssq = sm1[0]
            for j4 in range(4):
                k.op("dve", lambda: V.bn_stats(out=bst[0:P, j4 * 6:(j4 + 1) * 6], in_=xb.sl(P, j4 * 512, (j4 + 1) * 512)), cover, [bst])
            k.op("dve", lambda: V.bn_aggr(out=ssq4[0:P, 0:2], in_=bst[0:P, :]), [bst], [ssq4])
            k.op("dve", lambda: V.scalar_tensor_tensor(out=ssq[0:P, 0:1], in0=ssq4[0:P, 0:1], scalar=ssq4[0:P, 0:1], in1=ssq4[0:P, 1:2],
                                                       op0=ALU.mult, op1=ALU.add), [ssq4], [ssq])
            rstd_from_ss(P, ssq[0:P, 0:1], ssq, 1.0, sm1[1], sm1[1][0:P, 0:1], sm1[2], sm1[2][0:P, 0:1])
            for j in range(4):
                xs_ = xb.sl(P, j * 512, (j + 1) * 512)
                k.op("dve", lambda: V.scalar_tensor_tensor(out=xs_, in0=xs_, scalar=sm1[1][0:P, 0:1], in1=gf[j][0:P, :], op0=ALU.mult, op1=ALU.mult),
                     cover + [sm1[1], gf[j]], cover)
            dst = yp[row0:row0 + P, :] if gt_ < NT // 128 else ys[:, :]
            dst_t = out_t["yp"] if gt_ < NT // 128 else out_t["ys"]
            for ap_, off_ in xb.parts:
                w_ = ap_.shape[1]
                k.dma("sp", dst[:, off_:off_ + w_], ap_[0:P, :], cover, [], pwrites=[dst_t])
        for ds in k.dsems:
            k._wait("sp", (ds[0], ds[1]))
    return nc


_CACHE = {}


def _consts():
    c = np.zeros((5, 128, 128), np.float32)
    c[0] = np.eye(128)
    c[1] = 1.0
    s = np.arange(128)[:, None]
    t = np.arange(128)[None, :]
    c[2] = (s <= t)
    c[3] = np.where(s <= t, 0.0, -30000.0)
    c[4][127, :] = 1.0
    return c


def kernel(x_prompt, x_sample, state_C, state_n, state_m, c_prompt, c_sample,
           g_norm, w_ada, b_ada, w_in, b_igate, b_fgate, g_mh, g_cmv, w_s, b_s, w_out, g_final):
    f = lambda a: np.ascontiguousarray(np.asarray(a), dtype=np.float32)
    x_prompt, x_sample, state_C, state_n, state_m = map(f, (x_prompt, x_sample, state_C, state_n, state_m))
    c_prompt, c_sample, g_norm, w_ada, b_ada, w_in = map(f, (c_prompt, c_sample, g_norm, w_ada, b_ada, w_in))
    b_igate, b_fgate, g_mh, g_cmv, w_s, b_s, w_out, g_final = map(f, (b_igate, b_fgate, g_mh, g_cmv, w_s, b_s, w_out, g_final))
    if "nc" not in _CACHE:
        _CACHE["nc"] = build_nc()
    nc = _CACHE["nc"]
    offs = dict(q=0, k=1024, v=2048, o=3072, z=4096, gi=5120, gf=5124, u=5128, vc=6152, zc=7176)
    cols = []
    for i in range(4):
        for a, b in (("q", "k"), ("v", "o"), ("z", "vc"), ("u", "zc")):
            cols += list(range(offs[a] + i * 256, offs[a] + (i + 1) * 256))
            cols += list(range(offs[b] + i * 256, offs[b] + (i + 1) * 256))
    cols = np.array(cols)
    w_inp = np.ascontiguousarray(w_in[:, :, cols])
    wgT = np.ascontiguousarray(w_in[:, :, 5120:5128].reshape(2, KC, 128, 8).transpose(0, 2, 1, 3))
    badaT = np.ascontiguousarray(b_ada[:, 0:4096].reshape(2, 32, 128).transpose(0, 2, 1))
    bgate = np.ascontiguousarray(b_ada[:, 4096:6144])
    gnT = np.ascontiguousarray(g_norm.reshape(2, KC, 128).transpose(0, 2, 1))
    big = np.ascontiguousarray(np.concatenate([b_igate, b_fgate], axis=1))
    wsT = np.ascontiguousarray(w_s.transpose(0, 1, 3, 2))
    bsT = np.ascontiguousarray(b_s.transpose(0, 2, 1))
    ws00 = np.ascontiguousarray(np.concatenate([w_s[:, :, 0, 0], b_s[:, :, 0]], axis=1))
    consts = _consts()
    in_maps = []
    for c in range(NCORES):
        b = c // 2
        s = c % 2
        if NHALF == 2:
            xp = x_prompt[b]
        else:
            xp = x_prompt[b, s * 1024:(s + 1) * 1024]
        sl = slice(c * NS, (c + 1) * NS)
        in_maps.append(dict(
            xp=np.ascontiguousarray(xp), xsm=np.ascontiguousarray(x_sample[sl, 0, :]),
            cvec=np.ascontiguousarray(np.concatenate([c_prompt[b:b + 1], c_sample[sl]], axis=0)),
            sC=np.ascontiguousarray(state_C[:, sl].reshape(2, NS, NHD, 128, 2, HD)),
            sn=np.ascontiguousarray(state_n[:, sl].reshape(2, NS, NHD * HD)),
            smm=np.ascontiguousarray(state_m[:, sl]),
            w_ada=w_ada, w_inp=w_inp, w_out=w_out, wgT=wgT, badaT=badaT, bgate=bgate, gnT=gnT, big=big,
            gmh=g_mh, gcmv=g_cmv, wsT=wsT, bsT=bsT, ws00=ws00, gfin=g_final.reshape(1, D), consts=consts, msk=np.full((1, 1), float(s), np.float32),
        ))
    res = run_bass_kernel_spmd(nc, in_maps, core_ids=list(range(NCORES)))
    R = res.results
    B = 4
    y_prompt = np.zeros((B, 2048, D), np.float32)
    y_sample = np.zeros((128, 1, D), np.float32)
    Cp = np.zeros((2, B, NHD, HD, HD), np.float32)
    npr = np.zeros((2, B, NHD, HD), np.float32)
    mp = np.zeros((2, B, NHD), np.float32)
    Cs = np.zeros((2, 128, NHD, HD, HD), np.float32)
    ns = np.zeros((2, 128, NHD, HD), np.float32)
    ms = np.zeros((2, 128, NHD), np.float32)
    vr = np.zeros((2, 128, 1, 1024), np.float32)
    for c in range(NCORES):
        b = c // 2
        s = c % 2
        r = R[c]
        sl = slice(c * NS, (c + 1) * NS)
        if NHALF == 2:
            if s == 0:
                y_prompt[b] = r["yp"]
        else:
            y_prompt[b, s * 1024:(s + 1) * 1024] = r["yp"]
        y_sample[sl, 0, :] = r["ys"]
        if (NHALF == 2 and s == 0) or (NHALF == 1 and s == 1):
            cp = r["Cp"].transpose(0, 1, 3, 2, 4).reshape(2, NHD, HD, HD)
            Cp[:, b] = cp.transpose(0, 1, 3, 2)
            npr[:, b] = r["npo"].transpose(0, 1, 3, 2).reshape(2, NHD, HD)
            mp[:, b] = r["mpo"]
        Cs[:, sl] = r["Cs"].reshape(2, NS, NHD, HD, HD)
        ns[:, sl] = r["nso"].reshape(2, NS, NHD, HD)
        ms[:, sl] = r["mso"]
        vr[:, sl, 0, :] = r["vro"]
    return (y_prompt, y_sample, Cp, npr, mp, Cs, ns, ms, vr)
```

```python
import numpy as np
from contextlib import ExitStack
import concourse.bass as bass
import concourse.mybir as mybir
from concourse.bass_utils import run_bass_kernel_spmd

F32 = mybir.dt.float32
BF16 = mybir.dt.bfloat16
AF = mybir.ActivationFunctionType
ALU = mybir.AluOpType
AX = mybir.AxisListType

D = 2048
KC = 16
NHD = 4
HD = 256
NS = 16
EPS = 1e-6
NHALF = 1
COLL = (NHALF == 1)
RG = [[0, 1], [2, 3], [4, 5], [6, 7]]
NT = NHALF * 1024
NCORES = 8


import os
KSTOP = float(os.environ.get('KSTOP', '99'))


class StopBuild(Exception):
    pass


def stage(n):
    if n >= KSTOP:
        raise StopBuild()


class Tl:
    def __init__(self, ap, name, bank=None):
        self.ap = ap
        self.name = name
        self.w = {}
        self.r = {}
        self.dsem = None
        self.bank = bank

    def __getitem__(self, idx):
        return self.ap[idx]


class K:
    def __init__(self, nc, stack):
        self.nc = nc
        self.stack = stack
        self.engs = {"pe": nc.tensor, "dve": nc.vector, "act": nc.scalar, "pool": nc.gpsimd, "sp": nc.sync}
        self.sem = {}
        self.cnt = {}
        self.seen = {e: {} for e in self.engs}
        for e in self.engs:
            self.sem[e] = stack.enter_context(nc.semaphore("s_" + e))
            self.cnt[e] = 0
        self.nsem = 0
        self.dsems = []
        self.pe_reads = []
        self.pe_writes = []
        self.flip = 0

    def sb(self, name, shape, dt):
        return Tl(self.stack.enter_context(self.nc.sbuf_tensor(name, shape, dt)), name)

    def ps(self, name, shape, dt):
        t = Tl(self.stack.enter_context(self.nc.psum_tensor(name, shape, dt)), name)
        t.bank = t
        return t

    def _wait(self, e, tok):
        if tok is None:
            return
        s, v = tok
        key = id(s)
        if self.seen[e].get(key, 0) >= v:
            return
        self.seen[e][key] = v
        self.engs[e].wait_ge(s, v)

    @staticmethod
    def _split(reads, writes):
        rr, ww = [], []
        for t in reads:
            if t is None:
                continue
            if t.bank is not None:
                if t.bank not in ww:
                    ww.append(t.bank)
            elif t not in rr:
                rr.append(t)
        for t in writes:
            if t is None:
                continue
            t = t.bank if t.bank is not None else t
            if t not in ww:
                ww.append(t)
        return rr, ww

    def deps(self, e, reads, writes):
        reads, writes = self._split(reads, writes)
        for t in reads:
            for tok in t.w.values():
                self._wait(e, tok)
        for t in writes:
            for tok in t.w.values():
                self._wait(e, tok)
            for tok in t.r.values():
                self._wait(e, tok)

    def reg(self, tok, reads, writes):
        reads, writes = self._split(reads, writes)
        sid = id(tok[0])
        for t in reads:
            if t.r.get(sid, (None, 0))[1] < tok[1]:
                t.r[sid] = tok
        for t in writes:
            if t.w.get(sid, (None, 0))[1] < tok[1]:
                t.w[sid] = tok

    def op(self, e, fn, reads, writes):
        self.deps(e, reads, writes)
        ins = fn()
        self.cnt[e] += 1
        ins.then_inc(self.sem[e], 1)
        tok = (self.sem[e], self.cnt[e])
        self.reg(tok, reads, writes)
        return tok

    def mm(self, out_tl, out_ap, lhsT_tl, lhsT_ap, rhs_tl, rhs_ap, start, stop, last=None, transpose=False):
        reads = [t for t in (lhsT_tl, rhs_tl) if t is not None]
        self.deps("pe", reads, [out_tl] if start else [])
        if transpose:
            ins = self.nc.tensor.transpose(out_ap, lhsT_ap, rhs_ap)
        else:
            ins = self.nc.tensor.matmul(out_ap, lhsT_ap, rhs_ap, start=start, stop=stop)
        self.pe_reads += reads
        if out_tl not in self.pe_writes:
            self.pe_writes.append(out_tl)
        if last is None:
            last = stop
        if last:
            self.cnt["pe"] += 1
            ins.then_inc(self.sem["pe"], 1)
            tok = (self.sem["pe"], self.cnt["pe"])
            self.reg(tok, self.pe_reads, self.pe_writes)
            self.pe_reads = []
            self.pe_writes = []
            return tok
        return None

    def dma(self, e, out_ap, in_ap, reads, writes, pwrites=(), **kw):
        pw = list(pwrites)
        reads = [t for t in reads if t is not None]
        self.deps(e, reads, writes)
        cands = [t for t in list(writes) if t.ap is not None] or [t for t in reads if t.ap is not None] or (list(writes) + pw)
        owner = cands[0]
        if owner.dsem is None:
            self.nsem += 1
            owner.dsem = [self.stack.enter_context(self.nc.semaphore("d%d" % self.nsem)), 0]
            self.dsems.append(owner.dsem)
        ins = self.engs[e].dma_start(out=out_ap, in_=in_ap, **kw)
        owner.dsem[1] += 16
        ins.then_inc(owner.dsem[0], 16)
        tok = (owner.dsem[0], owner.dsem[1])
        self.reg(tok, reads, list(writes) + pw)
        return tok

    def cc(self, in_ap, out_ap, reads, writes):
        self.deps("pool", reads, writes)
        if not hasattr(self, "ccsem"):
            self.ccsem = [self.stack.enter_context(self.nc.semaphore("ccsem")), 0]
        ins = self.nc.gpsimd.collective_compute("AllGather", ALU.bypass, replica_groups=RG, ins=[in_ap.opt()], outs=[out_ap.opt()])
        self.ccsem[1] += 1
        ins.then_inc(self.ccsem[0])
        tok = (self.ccsem[0], self.ccsem[1])
        self.reg(tok, reads, writes)
        return tok

    def ve(self):
        self.flip ^= 1
        return "dve" if self.flip else "act"


def build_nc():
    nc = bass.Bass("TRN2", target_bir_lowering=False)

    def din(name, shape):
        return nc.dram_tensor(name, shape, F32, kind="ExternalInput").ap()

    def dout(name, shape):
        return nc.dram_tensor(name, shape, F32, kind="ExternalOutput").ap()

    xp = din("xp", [NT, D])
    xsm = din("xsm", [NS, D])
    cvec = din("cvec", [17, D])
    sC = din("sC", [2, NS, NHD, 128, 2, HD])
    sn = din("sn", [2, NS, NHD * HD])
    smm = din("smm", [2, NS, NHD])
    w_ada = din("w_ada", [2, D, 6144])
    w_inp = din("w_inp", [2, D, 8192])
    w_out = din("w_out", [2, D, D])
    wgT = din("wgT", [2, 128, KC, 8])
    badaT = din("badaT", [2, 128, 32])
    bgate = din("bgate", [2, D])
    gnT = din("gnT", [2, 128, KC])
    big = din("big", [2, 8])
    gmh = din("gmh", [2, 1024])
    gcmv = din("gcmv", [2, 1024])
    wsT = din("wsT", [2, 4, 128, 128])
    bsT = din("bsT", [2, 128, 4])
    ws00 = din("ws00", [2, 8])
    gfin = din("gfin", [1, D])
    consts = din("consts", [5, 128, 128])
    msk = din("msk", [1, 1])

    yp = dout("yp", [NT, D])
    ys = dout("ys", [NS, D])
    Cp = dout("Cp", [2, NHD, 128, 2, HD])
    npo = dout("npo", [2, NHD, 128, 2])
    mpo = dout("mpo", [2, NHD])
    Cs = dout("Cs", [2, NS, NHD, 128, 2, HD])
    nso = dout("nso", [2, NS, NHD * HD])
    mso = dout("mso", [2, NS, NHD])
    vro = dout("vro", [2, NS, 1024])

    xs1 = nc.dram_tensor("xs1", [NT + NS, D], F32, kind="Internal").ap()
    xs2 = nc.dram_tensor("xs2", [NT + NS, D], F32, kind="Internal").ap()
    cinC = [[nc.dram_tensor("cinC_%d_%d" % (l, h), [128, 514], F32).ap() for h in range(NHD)] for l in range(2)]
    coutC = [[nc.dram_tensor("coutC_%d_%d" % (l, h), [256, 514], F32).ap() for h in range(NHD)] for l in range(2)]
    cinM = [nc.dram_tensor("cinM_%d" % l, [1, 4], F32).ap() for l in range(2)]
    qks = nc.dram_tensor("qks", [2, NHD, NS, 512], F32).ap()
    coutM = [nc.dram_tensor("coutM_%d" % l, [2, 4], F32).ap() for l in range(2)]

    with ExitStack() as st:
        k = K(nc, st)
        V, A, PE_ = nc.vector, nc.scalar, nc.tensor

        NRING = 3
        wring = [k.sb("wr%d" % i, [128, KC, 512], BF16) for i in range(NRING)]
        hT = k.sb("hT", [128, KC, 1152], BF16)
        mT = k.sb("mT", [128, KC, 1152], BF16)
        qT = k.sb("qT", [128, 2, 1024], BF16)
        kT = k.sb("kT", [128, 2, 1024], BF16)
        ktok = k.sb("ktok", [128, 8, 256], BF16)
        vtok = k.sb("vtok", [128, 8, 256], BF16)
        ogt = st.enter_context(nc.sbuf_tensor("og", [128, 9, 256], F32))
        og = [Tl(ogt[:, i, :], "og%d" % i) for i in range(9)]
        xtile_ap = ogt[:, 0:8, :].rearrange("p a b -> p (a b)")
        class XB:
            def __init__(self, parts, tls):
                self.parts = parts
                self.tls = tls

            def sl(self, P, lo, hi):
                for ap, off in self.parts:
                    w = ap.shape[1]
                    if off <= lo and hi <= off + w:
                        return ap[0:P, lo - off:hi - off]
                raise AssertionError((lo, hi))

        f32v = lambda t: t[:, :, :].rearrange("p a b -> p (a b)").bitcast(F32)
        xbs = [XB([(xtile_ap, 0)], og[0:8]),
               XB([(f32v(qT), 0), (f32v(kT), 1024)], [qT, kT]),
               XB([(f32v(ktok), 0), (f32v(vtok), 1024)], [ktok, vtok])]
        rr = k.sb("rr", [128, 9], F32)
        ssc = k.sb("ssc", [128, 8], F32)
        ssc2 = k.sb("ssc2", [128, 8], F32)
        CT32 = [k.sb("CT%d" % h, [128, 2, 256], F32) for h in range(NHD)]
        n32 = [k.sb("n32_%d" % h, [128, 2], F32) for h in range(NHD)]
        CTb = k.sb("CTb", [128, 2, 256], BF16)
        nb = k.sb("nb", [128, 2], BF16)
        cst = k.sb("cst", [128, 5, 128], F32)
        identb = k.sb("identb", [128, 128], BF16)
        onesb = k.sb("onesb", [128, 1], BF16)
        scT = k.sb("scT", [128, KC, 17], BF16)
        scTb = k.sb("scTb", [128, KC, 128], BF16)
        scTs = k.sb("scTs", [128, KC, 128], BF16)
        wg = k.sb("wg", [128, 2, KC, 8], BF16)
        adaT = k.sb("adaT", [128, 32, 17], F32)
        adaT2 = k.sb("adaT2", [128, 32, 17], F32)
        badaT_sb = k.sb("badaT_sb", [128, 2, 32], F32)
        gnT_sb = k.sb("gnT_sb", [128, 2, KC], F32)
        gsT = k.sb("gsT", [128, KC], F32)
        shP = k.sb("shP", [128, KC], F32)
        sc1T = k.sb("sc1T", [128, KC, NS], F32)
        shS = k.sb("shS", [128, KC, NS], F32)
        big_bc = k.sb("big_bc", [128, 2, 8], F32)
        ws00_bc = k.sb("ws00_bc", [128, 2, 8], F32)
        bsT_sb = k.sb("bsT_sb", [128, 2, 4], F32)
        wsTb = k.sb("wsTb", [128, 4, 128], BF16)
        gmh_bc = k.sb("gmh_bc", [128, 256], F32)
        gcmv_bc = k.sb("gcmv_bc", [128, 256], F32)
        gate_bc = k.sb("gate_bc", [128, 512], F32)
        gate_s = k.sb("gate_s", [128, 512], F32)
        wk = [k.sb("wk%d" % i, [128, 512], F32) for i in range(6)]
        wkb = [k.sb("wkb%d" % i, [128, 512], BF16) for i in range(2)]
        ssq4 = k.sb("ssq4", [128, 4], F32)
        sm1 = [k.sb("sm1_%d" % i, [128, 16], F32) for i in range(8)]
        gnames = ["IG", "XF", "LL", "BL", "U", "CU", "CUl", "BLl", "M", "WI", "ENM", "G", "DEC", "Ml", "T1", "GA", "DECA", "MlA", "NEGM"]
        gt = {n: k.sb("g_" + n, [128, 8, 4], F32) for n in gnames}
        mbc = k.sb("mbc", [128, 9, 4], F32)
        mbcA = k.sb("mbcA", [128, 9, 4], F32)
        msk_bc = k.sb("msk_bc", [128, 1], F32)
        min_bc = k.sb("min_bc", [128, 4], F32)
        UT = wk[0]
        CUT = wk[1]
        qk_s = k.sb("qk_s", [NS, 512], F32)
        v_s = k.sb("v_s", [NS, 256], F32)
        cq_tok = k.sb("cq_tok", [NS, 256], F32)
        n_s = k.sb("n_s", [NS, 256], F32)
        sg = {n: k.sb("sg_" + n, [NS, 4], F32) for n in ["ig", "xf", "ll", "a", "mt", "wa", "wi", "enm", "m0", "t"]}
        ssm = [k.sb("ssm%d" % i, [NS, 4], F32) for i in range(6)]
        dgw = k.sb("dgw", [NS, 2, NS], F32)
        wbc = k.sb("wbc", [128, 2, NS], F32)
        vsT = k.sb("vsT", [128, 2, NS], F32)
        cqT = k.sb("cqT", [128, 2, NS], F32)
        Cbuf = [k.sb("Cbuf%d" % i, [128, 2, 256], F32) for i in range(2)]
        pj = [k.ps("pj%d" % i, [128, 512], F32) for i in range(2)]
        ptb = k.ps("ptb", [128, 1024], BF16)
        pt32 = k.ps("pt32", [128, 512], F32)
        pm1t = st.enter_context(nc.psum_tensor("pm1", [128, 512], F32))
        bank_pm1 = Tl(None, "bank_pm1")
        pS = Tl(pm1t[:, 0:128], "pS", bank=bank_pm1)
        pbc = Tl(pm1t[:, 128:256], "pbc", bank=bank_pm1)
        psml = Tl(pm1t[:, 256:512], "psml", bank=bank_pm1)
        pnumt = st.enter_context(nc.psum_tensor("pnum", [128, 512], F32))
        bank_pnum = Tl(None, "bank_pnum")
        pinter = Tl(pnumt[:, 0:256], "pinter", bank=bank_pnum)
        pintra = Tl(pnumt[:, 256:512], "pintra", bank=bank_pnum)
        pupd = k.ps("pupd", [128, 2, 256], F32)
        psb = k.ps("psb", [128, 512], F32)
        pupd512 = Tl(pupd[:, :, :].rearrange("p a b -> p (a b)"), "pupd512", bank=pupd)

        ident = cst[:, 0, :]
        ones = cst[:, 1, :]
        tri = cst[:, 2, :]
        maskneg = cst[:, 3, :]
        e127 = cst[:, 4, :]

        def dtl(name):
            return Tl(None, name)
        xs_t = {}
        for nm in ("xs1", "xs2"):
            for tt in range(NT // 128 + 1):
                xs_t[(nm, tt)] = dtl("%s_%d" % (nm, tt))
        cinM_t = [dtl("cinM%d" % l) for l in range(2)]
        coutM_t = [dtl("coutM%d" % l) for l in range(2)]
        cinC_t = [[dtl("cinC") for h in range(NHD)] for l in range(2)]
        coutC_t = [[dtl("coutC") for h in range(NHD)] for l in range(2)]
        out_t = {n: dtl(n) for n in ["yp", "ys", "Cp", "npo", "mpo", "Cs", "nso", "mso", "vro"]}

        EARLY_ADA = COLL
        units = []
        for l in range(2):
            for half in range(NHALF):
                if half == 0 and not (EARLY_ADA and l == 1):
                    for u in range(8):
                        units.append(("ada", l, u))
                for i in range(16):
                    units.append(("inp", l, i))
                    if EARLY_ADA and l == 0 and i % 4 == 3:
                        units.append(("ada", 1, 2 * (i // 4)))
                        units.append(("ada", 1, 2 * (i // 4) + 1))
                for j in range(4):
                    units.append(("ada", l, 8 + j))
                    units.append(("out", l, j))
        wstate = {"loaded": 0, "next": 0}

        def wsrc(u):
            kind, l, b = u
            src = {"ada": w_ada, "inp": w_inp, "out": w_out}[kind]
            return src[l, :, b * 512:(b + 1) * 512].rearrange("(kc p) c -> p kc c", p=128)

        def wload(j):
            slot = wring[j % NRING]
            k.dma("pool", slot[:, :, :], wsrc(units[j]), [], [slot])

        def wacquire(expect):
            j = wstate["next"]
            assert units[j] == expect, (units[j], expect)
            while wstate["loaded"] < min(len(units), j + NRING) and wstate["loaded"] <= j:
                wload(wstate["loaded"])
                wstate["loaded"] += 1
            wstate["next"] += 1
            return wring[j % NRING], j

        def wrelease(j):
            nxt = j + NRING
            if nxt < len(units) and wstate["loaded"] == nxt:
                wload(nxt)
                wstate["loaded"] += 1

        for j in range(NRING):
            wload(j)
        wstate["loaded"] = NRING

        k.dma("sp", cst[:, :, :], consts.rearrange("c p f -> p c f"), [], [cst])
        k.op("dve", lambda: V.tensor_copy(out=identb[:, :], in_=ident), [cst], [identb])
        k.op("dve", lambda: V.memset(onesb[:, :], 1.0), [], [onesb])
        k.dma("sp", msk_bc[:, :], msk[0].partition_broadcast(128), [], [msk_bc])
        wgf = wk[0]
        k.dma("sp", wgf[:, 0:256].rearrange("p (l k c) -> p l k c", l=2, k=KC), wgT.rearrange("l p k c -> p l k c"), [], [wgf])
        k.op("dve", lambda: V.tensor_copy(out=wg[:, :, :, :], in_=wgf[:, 0:256].rearrange("p (l k c) -> p l k c", l=2, k=KC)), [wgf], [wg])
        k.dma("sp", badaT_sb[:, :, :], badaT.rearrange("l p c -> p l c"), [], [badaT_sb])
        k.dma("sp", gnT_sb[:, :, :], gnT.rearrange("l p c -> p l c"), [], [gnT_sb])
        k.dma("sp", bsT_sb[:, :, :], bsT.rearrange("l p c -> p l c"), [], [bsT_sb])
        for l in range(2):
            k.dma("sp", big_bc[:, l, :], big[l].partition_broadcast(128), [], [], pwrites=[big_bc])
            k.dma("sp", ws00_bc[:, l, :], ws00[l].partition_broadcast(128), [], [], pwrites=[ws00_bc])
        cv = wk[1]
        cvt = Tl(None, "cvt")
        for q4 in range(4):
            k.dma("sp", cv[0:17, :], cvec[:, q4 * 512:(q4 + 1) * 512], [], [cv])
            k.op("act", lambda: A.activation(out=cv[0:17, :], in_=cv[0:17, :], func=AF.Silu), [cv], [cv])
            for j in range(4):
                k.mm(pt32, pt32[:, j * 17:(j + 1) * 17], cv, cv[0:17, j * 128:(j + 1) * 128], cst, ident[0:17, 0:17],
                     start=True, stop=True, last=(j == 3), transpose=True)
            k.op("dve", lambda: V.tensor_copy(out=scT[:, q4 * 4:(q4 + 1) * 4, :],
                                               in_=pt32[:, 0:68].rearrange("p (a b) -> p a b", a=4)), [pt32], [scT])
        k.op("dve", lambda: V.tensor_copy(out=scTb[:, :, :], in_=scT[:, :, 0:1].to_broadcast([128, KC, 128])), [scT], [scTb])
        k.op("dve", lambda: V.memset(scTs[:, :, :], 0.0), [], [scTs])
        k.op("dve", lambda: V.tensor_copy(out=scTs[:, :, 0:NS], in_=scT[:, :, 1:17]), [scT], [scTs])
        k.op("dve", lambda: V.memset(hT[:, :, 1024:1152], 0.0), [], [hT])
        k.op("dve", lambda: V.memset(mT[:, :, 1024:1152], 0.0), [], [mT])

        def rstd_from_ss(P, ss_ap, ss_tl, n, out_tl, out_ap, tmp_tl, tmp_ap):
            k.op("dve", lambda: V.tensor_scalar(out=tmp_ap, in0=ss_ap, scalar1=1.0 / n, scalar2=EPS, op0=ALU.mult, op1=ALU.add),
                 [ss_tl], [tmp_tl])
            k.op("act", lambda: A.activation(out=tmp_ap, in_=tmp_ap, func=AF.Sqrt), [tmp_tl], [tmp_tl])
            k.op("dve", lambda: V.reciprocal(out=out_ap, in_=tmp_ap), [tmp_tl], [out_tl])

        def proj_tok(unit, tt, ntok, tok0, pjt):
            for kc in range(KC):
                k.mm(pjt, pjt[:, :], hT, hT[:, kc, tok0:tok0 + 128], unit, unit[:, kc, :], start=(kc == 0), stop=(kc == KC - 1))

        tiles_of_half = lambda half: [(c, 128, c * 128) for c in range(8)] + ([(8, NS, 1024)] if half == 0 else [])

        try:
          stage(0)
          for l in range(2):
              xin = (lambda tt, half: (xp[half * 1024 + tt * 128: half * 1024 + (tt + 1) * 128, :] if tt < 8 else xsm)) if l == 0 else \
                    (lambda tt, half: (xs1[half * 1024 + tt * 128: half * 1024 + (tt + 1) * 128, :] if tt < 8 else xs1[NT:NT + NS, :]))
              xin_t = (lambda tt, half: None) if l == 0 else (lambda tt, half: xs_t[("xs1", half * 8 + tt if tt < 8 else NT // 128)])
              xout = xs1 if l == 0 else xs2
              xout_nm = "xs1" if l == 0 else "xs2"

              for h in range(NHD):
                  k.op("dve", lambda: V.memset(CT32[h][:, :, :], 0.0), [], [CT32[h]])
                  k.op("dve", lambda: V.memset(n32[h][:, :], 0.0), [], [n32[h]])
              k.op("dve", lambda: V.memset(mbc[:, 8, :], 0.0), [], [mbc])

              for half in range(NHALF):
                  tiles = tiles_of_half(half)
                  def ada_unit(l_, u, dst):
                      unit, uj = wacquire(("ada", l_, u))
                      for j in range(4):
                          for kc in range(KC):
                              k.mm(pt32, pt32[:, j * 17:(j + 1) * 17], unit, unit[:, kc, j * 128:(j + 1) * 128], scT, scT[:, kc, :],
                                   start=(kc == 0), stop=(kc == KC - 1), last=(kc == KC - 1 and j == 3))
                      wrelease(uj)
                      k.op("dve", lambda: V.tensor_tensor(out=dst[:, u * 4:(u + 1) * 4, :],
                                                          in0=pt32[:, 0:68].rearrange("p (a b) -> p a b", a=4),
                                                          in1=badaT_sb[:, l_, u * 4:(u + 1) * 4].unsqueeze(2).to_broadcast([128, 4, 17]),
                                                          op=ALU.add), [pt32, badaT_sb], [dst])

                  if half == 0:
                      if EARLY_ADA and l == 1:
                          adaS = adaT2
                      else:
                          adaS = adaT
                          for u in range(8):
                              ada_unit(l, u, adaT)
                      k.op("dve", lambda: V.scalar_tensor_tensor(out=gsT[:, :], in0=adaS[:, 16:32, 0], scalar=1.0, in1=gnT_sb[:, l, :],
                                                                 op0=ALU.add, op1=ALU.mult), [adaS, gnT_sb], [gsT])
                      k.op("dve", lambda: V.tensor_copy(out=shP[:, :], in_=adaS[:, 0:16, 0]), [adaS], [shP])
                      k.op("dve", lambda: V.scalar_tensor_tensor(out=sc1T[:, :, :], in0=adaS[:, 16:32, 1:17], scalar=1.0,
                                                                 in1=gnT_sb[:, l, :].unsqueeze(2).to_broadcast([128, KC, NS]),
                                                                 op0=ALU.add, op1=ALU.mult), [adaS, gnT_sb], [sc1T])
                      k.op("dve", lambda: V.tensor_copy(out=shS[:, :, :], in_=adaS[:, 0:16, 1:17]), [adaS], [shS])

                  stage(1)
                  def xload(ti):
                      tt_, P_, tok0_ = tiles[ti]
                      xb_ = xbs[ti % 3]
                      src_t_ = xin_t(tt_, half)
                      for ap_, off_ in xb_.parts:
                          w_ = ap_.shape[1]
                          k.dma("sp", ap_[0:P_, :], xin(tt_, half)[:, off_:off_ + w_], [src_t_] if src_t_ else [], xb_.tls)

                  xload(0)
                  xload(1)
                  for ti, (tt, P, tok0) in enumerate(tiles):
                      if ti + 2 < len(tiles):
                          xload(ti + 2)
                      xb = xbs[ti % 3]
                      cover = xb.tls
                      ssq = sm1[0]
                      for j4 in range(4):
                          sqw = wk[4 + j4 % 2]
                          k.op("act", lambda: A.activation(out=sqw[0:P, :], in_=xb.sl(P, j4 * 512, (j4 + 1) * 512), func=AF.Square), cover, [sqw])
                          k.op("dve", lambda: V.tensor_reduce(out=ssq4[0:P, j4:j4 + 1], in_=sqw[0:P, :], axis=AX.X, op=ALU.add), [sqw], [ssq4])
                      k.op("dve", lambda: V.tensor_reduce(out=ssq[0:P, 0:1], in_=ssq4[0:P, :], axis=AX.X, op=ALU.add), [ssq4], [ssq])
                      rstd_from_ss(P, ssq[0:P, 0:1], ssq, float(D), sm1[1], sm1[1][0:P, 0:1], sm1[2], sm1[2][0:P, 0:1])
                      for ap_, off_ in xb.parts:
                          k.op("dve", lambda: V.tensor_scalar(out=ap_[0:P, :], in0=ap_[0:P, :], scalar1=sm1[1][0:P, 0:1], scalar2=None,
                                                              op0=ALU.mult), cover + [sm1[1]], cover)
                      if tt < 8:
                          for g4 in range(4):
                              for j in range(4):
                                  kc = g4 * 4 + j
                                  k.mm(pt32, pt32[:, j * 128:(j + 1) * 128], xb.tls[0], xb.sl(128, kc * 128, (kc + 1) * 128), cst, ident,
                                       start=True, stop=True, last=(j == 3), transpose=True)
                              for j in range(4):
                                  kc = g4 * 4 + j
                                  e = k.ve()
                                  if e == "dve":
                                      k.op("dve", lambda: V.tensor_scalar(out=hT[:, kc, tok0:tok0 + 128], in0=pt32[:, j * 128:(j + 1) * 128],
                                                                          scalar1=gsT[:, kc:kc + 1], scalar2=shP[:, kc:kc + 1],
                                                                          op0=ALU.mult, op1=ALU.add), [pt32, gsT, shP], [hT])
                                  else:
                                      k.op("act", lambda: A.activation(out=hT[:, kc, tok0:tok0 + 128], in_=pt32[:, j * 128:(j + 1) * 128],
                                                                       func=AF.Identity, scale=gsT[:, kc:kc + 1], bias=shP[:, kc:kc + 1]),
                                           [pt32, gsT, shP], [hT])
                      else:
                          for kc in range(KC):
                              k.mm(pt32, pt32[:, kc * NS:(kc + 1) * NS], xb.tls[0], xb.sl(NS, kc * 128, (kc + 1) * 128), cst, ident[0:NS, 0:NS],
                                   start=True, stop=True, last=(kc == KC - 1), transpose=True)
                          t_ = wk[2]
                          k.op("dve", lambda: V.tensor_tensor(out=t_[:, 0:256].rearrange("p (a b) -> p a b", a=KC),
                                                              in0=pt32[:, 0:256].rearrange("p (a b) -> p a b", a=KC), in1=sc1T[:, :, :], op=ALU.mult),
                               [pt32, sc1T], [t_])
                          k.op("dve", lambda: V.tensor_tensor(out=hT[:, :, 1024:1040], in0=t_[:, 0:256].rearrange("p (a b) -> p a b", a=KC),
                                                              in1=shS[:, :, :], op=ALU.add), [t_, shS], [hT])

                  stage(2)
                  gp = psml
                  for (tt, P, tok0) in tiles:
                      if tt == 8:
                          continue
                      for kc in range(KC):
                          k.mm(gp, gp[:, tt * 8:(tt + 1) * 8], hT, hT[:, kc, tok0:tok0 + 128], wg, wg[:, l, kc, :],
                               start=(kc == 0), stop=(kc == KC - 1), last=(kc == KC - 1 and tt == 7))
                  gpv = gp[:, 0:64].rearrange("p (c e) -> p c e", c=8)
                  bi_b = big_bc[:, l, 0:4].unsqueeze(1).to_broadcast([128, 8, 4])
                  bf_b = big_bc[:, l, 4:8].unsqueeze(1).to_broadcast([128, 8, 4])
                  G_ = gt
                  k.op("dve", lambda: V.tensor_tensor(out=G_["IG"][:, :, :], in0=gpv[:, :, 0:4], in1=bi_b, op=ALU.add), [gp, big_bc], [G_["IG"]])
                  k.op("dve", lambda: V.tensor_tensor(out=G_["XF"][:, :, :], in0=gpv[:, :, 4:8], in1=bf_b, op=ALU.add), [gp, big_bc], [G_["XF"]])
                  k.op("act", lambda: A.activation(out=G_["XF"][:, :, :], in_=G_["XF"][:, :, :], func=AF.Exp, scale=-1.0), [G_["XF"]], [G_["XF"]])
                  k.op("act", lambda: A.activation(out=G_["LL"][:, :, :], in_=G_["XF"][:, :, :], func=AF.Ln, bias=1.0), [G_["XF"]], [G_["LL"]])
                  k.mm(pbc, pbc[:, 0:32], cst, tri, G_["LL"], G_["LL"][:, :, :].rearrange("p c h -> p (c h)"), start=True, stop=True)
                  k.op("dve", lambda: V.tensor_copy(out=G_["BL"][:, :, :].rearrange("p c h -> p (c h)"), in_=pbc[:, 0:32]), [pbc], [G_["BL"]])
                  k.op("dve", lambda: V.tensor_tensor(out=G_["U"][:, :, :], in0=G_["IG"][:, :, :], in1=G_["BL"][:, :, :], op=ALU.add),
                       [G_["IG"], G_["BL"]], [G_["U"]])
                  k.mm(pt32, pt32[0:32, 0:128], G_["U"], G_["U"][:, :, :].rearrange("p c h -> p (c h)"), cst, ident, start=True, stop=True, transpose=True)
                  k.op("dve", lambda: V.tensor_copy(out=UT[0:32, 0:128], in_=pt32[0:32, 0:128]), [pt32], [UT])
                  k.op("dve", lambda: V.tensor_tensor_scan(out=CUT[0:32, 0:128], data0=UT[0:32, 0:128], data1=UT[0:32, 0:128], initial=-1e30, op0=ALU.max, op1=ALU.max),
                       [UT], [CUT])
                  k.mm(pt32, pt32[:, 128:160], CUT, CUT[0:32, 0:128], cst, ident[0:32, 0:32], start=True, stop=True, transpose=True)
                  k.op("dve", lambda: V.tensor_copy(out=G_["CU"][:, :, :].rearrange("p c h -> p (c h)"), in_=pt32[:, 128:160]), [pt32], [G_["CU"]])
                  k.mm(pbc, pbc[:, 32:64], cst, e127, G_["CU"], G_["CU"][:, :, :].rearrange("p c h -> p (c h)"), start=True, stop=True)
                  k.op("dve", lambda: V.tensor_copy(out=G_["CUl"][:, :, :].rearrange("p c h -> p (c h)"), in_=pbc[:, 32:64]), [pbc], [G_["CUl"]])
                  k.mm(pbc, pbc[:, 64:96], cst, e127, G_["BL"], G_["BL"][:, :, :].rearrange("p c h -> p (c h)"), start=True, stop=True)
                  k.op("dve", lambda: V.tensor_copy(out=G_["BLl"][:, :, :].rearrange("p c h -> p (c h)"), in_=pbc[:, 64:96]), [pbc], [G_["BLl"]])
                  def mchain(mb, Ml):
                      for c in range(8):
                          k.op("dve", lambda: V.tensor_tensor(out=Ml[:, c, :], in0=mb[:, c, :], in1=G_["CUl"][:, c, :], op=ALU.max),
                               [mb, G_["CUl"]], [Ml])
                          k.op("dve", lambda: V.tensor_tensor(out=mb[:, c + 1, :], in0=Ml[:, c, :], in1=G_["BLl"][:, c, :], op=ALU.subtract),
                               [Ml, G_["BLl"]], [mb])

                  def chainB():
                      if COLL:
                          k.dma("sp", min_bc[:, :], coutM[l][0].partition_broadcast(128), [coutM_t[l]], [min_bc])
                          k.op("dve", lambda: V.tensor_scalar(out=mbc[:, 0, :], in0=min_bc[:, :], scalar1=msk_bc[:, 0:1], scalar2=None, op0=ALU.mult),
                               [min_bc, msk_bc], [mbc])
                      else:
                          k.op("dve", lambda: V.tensor_copy(out=mbc[:, 0, :], in_=mbc[:, 8, :]), [mbc], [mbc])
                      mchain(mbc, G_["Ml"])
                      mprev = mbc[:, 0:8, :]
                      k.op("dve", lambda: V.tensor_tensor(out=G_["M"][:, :, :], in0=G_["CU"][:, :, :], in1=mprev, op=ALU.max), [G_["CU"], mbc], [G_["M"]])
                      k.op("dve", lambda: V.tensor_scalar(out=G_["NEGM"][:, :, :], in0=G_["M"][:, :, :], scalar1=-1.0, scalar2=None, op0=ALU.mult), [G_["M"]], [G_["NEGM"]])
                      k.op("dve", lambda: V.tensor_tensor(out=G_["T1"][:, :, :], in0=mprev, in1=G_["M"][:, :, :], op=ALU.subtract), [mbc, G_["M"]], [G_["T1"]])
                      k.op("act", lambda: A.activation(out=G_["WI"][:, :, :], in_=G_["T1"][:, :, :], func=AF.Exp), [G_["T1"]], [G_["WI"]])
                      k.op("dve", lambda: V.tensor_tensor(out=G_["T1"][:, :, :], in0=G_["BL"][:, :, :], in1=G_["M"][:, :, :], op=ALU.subtract),
                           [G_["BL"], G_["M"]], [G_["T1"]])
                      k.op("act", lambda: A.activation(out=G_["ENM"][:, :, :], in_=G_["T1"][:, :, :], func=AF.Exp), [G_["T1"]], [G_["ENM"]])
                      k.op("dve", lambda: V.tensor_tensor(out=G_["T1"][:, :, :], in0=G_["U"][:, :, :], in1=G_["Ml"][:, :, :], op=ALU.subtract),
                           [G_["U"], G_["Ml"]], [G_["T1"]])
                      k.op("act", lambda: A.activation(out=G_["G"][:, :, :], in_=G_["T1"][:, :, :], func=AF.Exp), [G_["T1"]], [G_["G"]])
                      k.op("dve", lambda: V.tensor_tensor(out=G_["T1"][:, :, :], in0=mprev, in1=G_["Ml"][:, :, :], op=ALU.subtract), [mbc, G_["Ml"]], [G_["T1"]])
                      k.op("act", lambda: A.activation(out=G_["DEC"][:, :, :], in_=G_["T1"][:, :, :], func=AF.Exp), [G_["T1"]], [G_["DEC"]])
                      if half == NHALF - 1:
                          k.dma("sp", mpo[l:l + 1, :], mbc[0:1, 8, :], [mbc], [], pwrites=[out_t["mpo"]])

                  if COLL:
                      k.op("dve", lambda: V.memset(mbcA[:, 0, :], 0.0), [], [mbcA])
                      mchain(mbcA, G_["MlA"])
                      k.op("dve", lambda: V.tensor_tensor(out=G_["T1"][:, :, :], in0=G_["U"][:, :, :], in1=G_["MlA"][:, :, :], op=ALU.subtract),
                           [G_["U"], G_["MlA"]], [G_["T1"]])
                      k.op("act", lambda: A.activation(out=G_["GA"][:, :, :], in_=G_["T1"][:, :, :], func=AF.Exp), [G_["T1"]], [G_["GA"]])
                      k.op("dve", lambda: V.tensor_tensor(out=G_["T1"][:, :, :], in0=mbcA[:, 0:8, :], in1=G_["MlA"][:, :, :], op=ALU.subtract),
                           [mbcA, G_["MlA"]], [G_["T1"]])
                      k.op("act", lambda: A.activation(out=G_["DECA"][:, :, :], in_=G_["T1"][:, :, :], func=AF.Exp), [G_["T1"]], [G_["DECA"]])
                      k.dma("sp", cinM[l], mbcA[0:1, 8, :], [mbcA], [cinM_t[l]])
                      k.cc(cinM[l], coutM[l], [cinM_t[l]], [coutM_t[l]])
                  else:
                      chainB()
                  stage(3)
                  if half == 0:
                      for kc in range(KC):
                          k.mm(gp, gp[:, 64:72], hT, hT[:, kc, 1024:1152], wg, wg[:, l, kc, :], start=(kc == 0), stop=(kc == KC - 1))
                      stage(3.1)
                      S_ = sg
                      k.dma("sp", S_["m0"][:, :], smm[l], [], [S_["m0"]])
                      stage(3.2)
                      k.op("dve", lambda: V.tensor_tensor(out=S_["ig"][:, :], in0=gp[0:NS, 64:68], in1=big_bc[0:NS, l, 0:4], op=ALU.add), [gp, big_bc], [S_["ig"]])
                      k.op("dve", lambda: V.tensor_tensor(out=S_["xf"][:, :], in0=gp[0:NS, 68:72], in1=big_bc[0:NS, l, 4:8], op=ALU.add), [gp, big_bc], [S_["xf"]])
                      k.op("act", lambda: A.activation(out=S_["xf"][:, :], in_=S_["xf"][:, :], func=AF.Exp, scale=-1.0), [S_["xf"]], [S_["xf"]])
                      k.op("act", lambda: A.activation(out=S_["ll"][:, :], in_=S_["xf"][:, :], func=AF.Ln, bias=1.0), [S_["xf"]], [S_["ll"]])
                      k.op("dve", lambda: V.tensor_tensor(out=S_["a"][:, :], in0=S_["m0"][:, :], in1=S_["ll"][:, :], op=ALU.subtract), [S_["m0"], S_["ll"]], [S_["a"]])
                      k.op("dve", lambda: V.tensor_tensor(out=S_["mt"][:, :], in0=S_["a"][:, :], in1=S_["ig"][:, :], op=ALU.max), [S_["a"], S_["ig"]], [S_["mt"]])
                      k.op("dve", lambda: V.tensor_tensor(out=S_["t"][:, :], in0=S_["ig"][:, :], in1=S_["mt"][:, :], op=ALU.subtract), [S_["ig"], S_["mt"]], [S_["t"]])
                      k.op("act", lambda: A.activation(out=S_["wa"][:, :], in_=S_["t"][:, :], func=AF.Exp), [S_["t"]], [S_["wa"]])
                      k.op("dve", lambda: V.tensor_tensor(out=S_["t"][:, :], in0=S_["a"][:, :], in1=S_["mt"][:, :], op=ALU.subtract), [S_["a"], S_["mt"]], [S_["t"]])
                      k.op("act", lambda: A.activation(out=S_["wi"][:, :], in_=S_["t"][:, :], func=AF.Exp), [S_["t"]], [S_["wi"]])
                      k.op("act", lambda: A.activation(out=S_["enm"][:, :], in_=S_["mt"][:, :], func=AF.Exp, scale=-1.0), [S_["mt"]], [S_["enm"]])
                      stage(3.3)
                      k.dma("sp", mso[l], S_["mt"][:, :], [S_["mt"]], [], pwrites=[out_t["mso"]])

                  stage(4)
                  for i in range(NHD):
                      unit, uj = wacquire(("inp", l, 4 * i + 0))
                      pji = 0
                      for j in range(4):
                          for nbk in range(2):
                              pjt = pj[pji % 2]
                              pji += 1
                              for kc in range(KC):
                                  k.mm(pjt, pjt[:, :], unit, unit[:, kc, j * 128:(j + 1) * 128], hT, hT[:, kc, nbk * 512:(nbk + 1) * 512],
                                       start=(kc == 0), stop=(kc == KC - 1))
                              if j < 2:
                                  e = k.ve()
                                  if e == "dve":
                                      k.op("dve", lambda: V.tensor_copy(out=qT[:, j, nbk * 512:(nbk + 1) * 512], in_=pjt[:, :]), [pjt], [qT])
                                  else:
                                      k.op("act", lambda: A.copy(out=qT[:, j, nbk * 512:(nbk + 1) * 512], in_=pjt[:, :]), [pjt], [qT])
                              else:
                                  k.op("act", lambda: A.mul(out=kT[:, j - 2, nbk * 512:(nbk + 1) * 512], in_=pjt[:, :], mul=0.0625), [pjt], [kT])
                      if half == 0:
                          pjt = pj[pji % 2]
                          pji += 1
                          proj_tok(unit, 8, NS, 1024, pjt)
                          k.op("dve", lambda: V.tensor_copy(out=qk_s[:, 0:256], in_=pjt[0:NS, 0:256]), [pjt], [qk_s])
                          k.op("act", lambda: A.mul(out=qk_s[:, 256:512], in_=pjt[0:NS, 256:512], mul=0.0625), [pjt], [qk_s])
                      wrelease(uj)
                      stage(5)
                      for g2 in range(2):
                          for c4 in range(4):
                              c = g2 * 4 + c4
                              for dt_ in range(2):
                                  k.mm(ptb, ptb[:, (c4 * 2 + dt_) * 128:(c4 * 2 + dt_ + 1) * 128], kT, kT[:, dt_, c * 128:(c + 1) * 128], identb, identb[:, :],
                                       start=True, stop=True, last=(c4 == 3 and dt_ == 1), transpose=True)
                          k.op(k.ve(), (lambda: V.tensor_copy(out=ktok[:, g2 * 4:(g2 + 1) * 4, :].rearrange("p a b -> p (a b)"), in_=ptb[:, :])) if k.flip
                               else (lambda: A.copy(out=ktok[:, g2 * 4:(g2 + 1) * 4, :].rearrange("p a b -> p (a b)"), in_=ptb[:, :])), [ptb], [ktok])
                      unit, uj = wacquire(("inp", l, 4 * i + 1))
                      for (tt, P, tok0) in tiles:
                          pjt = pj[pji % 2]
                          pji += 1
                          proj_tok(unit, tt, P, tok0, pjt)
                          if tt < 8:
                              k.op("dve", lambda: V.tensor_copy(out=vtok[:, tt, :], in_=pjt[:, 0:256]), [pjt], [vtok])
                          else:
                              k.op("dve", lambda: V.tensor_copy(out=v_s[:, :], in_=pjt[0:NS, 0:256]), [pjt], [v_s])
                          k.op("act", lambda: A.activation(out=og[tt][0:P, :], in_=pjt[0:P, 256:512], func=AF.Sigmoid), [pjt], [og[tt]])
                      wrelease(uj)

                      stage(6)
                      def passB():
                          k.op("act", lambda: A.copy(out=CTb[:, :, :], in_=CT32[i][:, :, :]), [CT32[i]], [CTb])
                          k.op("dve", lambda: V.tensor_copy(out=nb[:, :], in_=n32[i][:, :]), [n32[i]], [nb])
                          def headA(c):
                              cs = slice(c * 128, (c + 1) * 128)
                              Dg, Wt, Pt, kg = wk[0], wk[1], wkb[0], wkb[1]
                              k.op("act", lambda: A.activation(out=Dg[:, 0:128], in_=ident, func=AF.Identity, scale=G_["NEGM"][:, c, i:i + 1]), [cst, G_["NEGM"]], [Dg])
                              k.op("act", lambda: A.activation(out=kg[:, 0:256], in_=ktok[:, c, :], func=AF.Identity, scale=G_["G"][:, c, i:i + 1]),
                                   [ktok, G_["G"]], [kg])
                              k.mm(pbc, pbc[:, :], cst, ones, Dg, Dg[:, 0:128], start=True, stop=False)
                              k.mm(pbc, pbc[:, :], cst, ident, cst, maskneg, start=False, stop=True)
                              for dt_ in range(2):
                                  k.mm(pS, pS[:, :], kT, kT[:, dt_, cs], qT, qT[:, dt_, cs], start=(dt_ == 0), stop=(dt_ == 1))
                              k.op("act", lambda: A.activation(out=Wt[:, 0:128], in_=pbc[:, :], func=AF.Exp, bias=G_["U"][:, c, i:i + 1]), [pbc, G_["U"]], [Wt])

                          def headB(c):
                              cs = slice(c * 128, (c + 1) * 128)
                              Dg, Wt, Pt, kg = wk[0], wk[1], wkb[0], wkb[1]
                              k.op("dve", lambda: V.tensor_tensor(out=Pt[:, 0:128], in0=pS[:, :], in1=Wt[:, 0:128], op=ALU.mult), [pS, Wt], [Pt])
                              for dt_ in range(2):
                                  k.mm(pinter, pinter[:, :], qT, qT[:, dt_, cs], CTb, CTb[:, dt_, :], start=(dt_ == 0), stop=(dt_ == 1))
                              for dt_ in range(2):
                                  k.mm(psml, psml[:, 80:81], qT, qT[:, dt_, cs], nb, nb[:, dt_:dt_ + 1], start=(dt_ == 0), stop=(dt_ == 1))
                              k.mm(pintra, pintra[:, :], Pt, Pt[:, 0:128], vtok, vtok[:, c, :], start=True, stop=True)
                              k.mm(psml, psml[:, 82:83], Pt, Pt[:, 0:128], onesb, onesb[:, :], start=True, stop=True)
                              for dt_ in range(2):
                                  k.mm(pupd, pupd[:, dt_, :], kg, kg[:, dt_ * 128:(dt_ + 1) * 128], vtok, vtok[:, c, :], start=True, stop=True, last=(dt_ == 1))
                              for dt_ in range(2):
                                  k.mm(psml, psml[:, 84 + dt_:85 + dt_], kg, kg[:, dt_ * 128:(dt_ + 1) * 128], onesb, onesb[:, :], start=True, stop=True, last=(dt_ == 1))

                          def bufs(c):
                              if c % 2 == 0:
                                  return wk[3], sm1[3], sm1[4], sm1[5]
                              return wk[5], sm1[0], sm1[1], sm1[2]

                          def tail1(c):
                              num, d1, d2, d3 = bufs(c)
                              intra_sb = wk[2]
                              k.op("act", lambda: A.copy(out=intra_sb[:, 0:256], in_=pintra[:, :]), [pintra], [intra_sb])
                              k.op("dve", lambda: V.scalar_tensor_tensor(out=num[:, 0:256], in0=pinter[:, :], scalar=G_["WI"][:, c, i:i + 1], in1=intra_sb[:, 0:256],
                                                                         op0=ALU.mult, op1=ALU.add), [pinter, G_["WI"], intra_sb], [num])
                              k.op("dve", lambda: V.tensor_copy(out=d1[:, 0:1], in_=psml[:, 82:83]), [psml], [d1])
                              k.op("dve", lambda: V.scalar_tensor_tensor(out=d2[:, 0:1], in0=psml[:, 80:81], scalar=G_["WI"][:, c, i:i + 1], in1=d1[:, 0:1],
                                                                         op0=ALU.mult, op1=ALU.add), [psml, G_["WI"], d1], [d2])
                              k.op("dve", lambda: V.scalar_tensor_tensor(out=CT32[i][:, :, :].rearrange("p a b -> p (a b)"),
                                                                         in0=CT32[i][:, :, :].rearrange("p a b -> p (a b)"), scalar=G_["DEC"][:, c, i:i + 1],
                                                                         in1=pupd[:, :, :].rearrange("p a b -> p (a b)"), op0=ALU.mult, op1=ALU.add),
                                   [CT32[i], G_["DEC"], pupd], [CT32[i]])
                              k.op("dve", lambda: V.scalar_tensor_tensor(out=n32[i][:, :], in0=n32[i][:, :], scalar=G_["DEC"][:, c, i:i + 1], in1=psml[:, 84:86],
                                                                         op0=ALU.mult, op1=ALU.add), [n32[i], G_["DEC"], psml], [n32[i]])
                              k.op("act", lambda: A.copy(out=CTb[:, :, :], in_=CT32[i][:, :, :]), [CT32[i]], [CTb])
                              k.op("dve", lambda: V.tensor_copy(out=nb[:, :], in_=n32[i][:, :]), [n32[i]], [nb])

                          def tail2(c):
                              num, d1, d2, d3 = bufs(c)
                              k.op("dve", lambda: V.tensor_scalar(out=d2[:, 1:2], in0=d2[:, 0:1], scalar1=-1.0, scalar2=None, op0=ALU.mult), [d2], [d2])
                              k.op("dve", lambda: V.tensor_tensor(out=d2[:, 0:1], in0=d2[:, 0:1], in1=d2[:, 1:2], op=ALU.max), [d2], [d2])
                              k.op("dve", lambda: V.tensor_tensor(out=d2[:, 0:1], in0=d2[:, 0:1], in1=G_["ENM"][:, c, i:i + 1], op=ALU.max), [d2, G_["ENM"]], [d2])
                              k.op("dve", lambda: V.reciprocal(out=d3[:, 0:1], in_=d2[:, 0:1]), [d2], [d3])
                              k.op("dve", lambda: V.scalar_tensor_tensor(out=og[c][:, :], in0=num[:, 0:256], scalar=d3[:, 0:1], in1=og[c][:, :],
                                                                         op0=ALU.mult, op1=ALU.mult), [num, d3, og[c]], [og[c]])
                              sq = wk[4]
                              k.op("act", lambda: A.activation(out=sq[:, 0:256], in_=og[c][:, :], func=AF.Square), [og[c]], [sq])
                              k.op("dve", lambda: V.tensor_reduce(out=ssc[:, c:c + 1], in_=sq[:, 0:256], axis=AX.X, op=ALU.add), [sq], [ssc])

                          headA(0)
                          headB(0)
                          tail1(0)
                          yield "pro"
                          for c in range(8):
                              if c < 7:
                                  headA(c + 1)
                              tail2(c)
                              yield "a"
                              if c < 7:
                                  headB(c + 1)
                                  yield "b"
                                  tail1(c + 1)
                              yield "c"
                          rstd_from_ss(128, ssc[:, 0:8], ssc, float(HD), rr, rr[:, 0:8], ssc2, ssc2[:, 0:8])
                          if half == NHALF - 1:
                              k.dma("sp", Cp[l, i], CT32[i][:, :, :], [CT32[i]], [], pwrites=[out_t["Cp"]])
                              k.dma("sp", npo[l, i], n32[i][:, :], [n32[i]], [], pwrites=[out_t["npo"]])


                      def sampleM():
                          if half == 0:
                              S_ = sg
                              k.dma("sp", n_s[:, :], sn[l, :, i * HD:(i + 1) * HD], [], [n_s])
                              nq, qk, s_, den, rec = ssm[0], ssm[1], ssm[2], ssm[3], ssm[4]
                              jk = wk[5]
                              k.op("dve", lambda: V.tensor_tensor(out=jk[0:NS, 0:256], in0=n_s[:, :], in1=qk_s[:, 0:256], op=ALU.mult), [n_s, qk_s], [jk])
                              k.op("dve", lambda: V.tensor_reduce(out=nq[:, 0:1], in_=jk[0:NS, 0:256], axis=AX.X, op=ALU.add), [jk], [nq])
                              k.op("dve", lambda: V.tensor_tensor(out=jk[0:NS, 0:256], in0=qk_s[:, 256:512], in1=qk_s[:, 0:256], op=ALU.mult), [qk_s], [jk])
                              k.op("dve", lambda: V.tensor_reduce(out=qk[:, 0:1], in_=jk[0:NS, 0:256], axis=AX.X, op=ALU.add), [jk], [qk])
                              k.op("dve", lambda: V.tensor_tensor(out=s_[:, 0:1], in0=qk[:, 0:1], in1=S_["wa"][:, i:i + 1], op=ALU.mult), [qk, S_["wa"]], [s_])
                              k.op("dve", lambda: V.tensor_scalar(out=dgw[:, 0, :], in0=ident[0:NS, 0:NS], scalar1=S_["wi"][:, i:i + 1], scalar2=None, op0=ALU.mult),
                                   [cst, S_["wi"]], [dgw])
                              k.op("dve", lambda: V.tensor_scalar(out=dgw[:, 1, :], in0=ident[0:NS, 0:NS], scalar1=S_["wa"][:, i:i + 1], scalar2=None, op0=ALU.mult),
                                   [cst, S_["wa"]], [dgw])
                              k.mm(pbc, pbc[:, 0:32], cst, ones[0:NS, :], dgw, dgw[:, :, :].rearrange("p a b -> p (a b)"), start=True, stop=True)
                              k.op("dve", lambda: V.tensor_copy(out=wbc[:, :, :].rearrange("p a b -> p (a b)"), in_=pbc[:, 0:32]), [pbc], [wbc])
                              for r in range(2):
                                  k.mm(pt32, pt32[:, r * NS:(r + 1) * NS], v_s, v_s[:, r:256:2], cst, ident[0:NS, 0:NS], start=True, stop=True,
                                       last=(r == 1), transpose=True)
                              k.op("dve", lambda: V.tensor_tensor(out=vsT[:, :, :], in0=pt32[:, 0:32].rearrange("p (a b) -> p a b", a=2),
                                                                  in1=wbc[:, 1:2, :].to_broadcast([128, 2, NS]), op=ALU.mult), [pt32, wbc], [vsT])
                              qks_t = Tl(None, "qks")
                              k.dma("sp", qks[l, i], qk_s[:, :], [qk_s], [], pwrites=[qks_t])

                              bcs = [gate_bc, gate_s]

                              def sloads(j):
                                  k.dma("sp", Cbuf[j % 2][:, :, :], sC[l, j, i], [], [Cbuf[j % 2]])
                                  k.dma("sp", bcs[j % 2][:, :], qks[l, i, j].partition_broadcast(128), [qks_t], [bcs[j % 2]])

                              sloads(0)
                              yield "pre"
                              for j in range(NS):
                                  cb = Cbuf[j % 2]
                                  bc = bcs[j % 2]
                                  if j + 1 < NS:
                                      sloads(j + 1)
                                  k.op("dve", lambda: V.tensor_tensor(out=psb[:, :].rearrange("p (a b) -> p a b", a=2), in0=cb[:, :, :],
                                                                      in1=bc[:, 0:256].unsqueeze(1).to_broadcast([128, 2, 256]), op=ALU.mult), [cb, bc], [psb])
                                  k.op("dve", lambda: V.tensor_reduce(out=cqT[:, :, j], in_=psb[:, :].rearrange("p (a b) -> p a b", a=2), axis=AX.X, op=ALU.add),
                                       [psb], [cqT])
                                  k.op("act", lambda: A.activation(out=cb[:, :, :], in_=cb[:, :, :], func=AF.Identity, scale=wbc[:, 0, j:j + 1]), [cb, wbc], [cb])
                                  for r in range(2):
                                      k.op("dve", lambda: V.scalar_tensor_tensor(out=cb[:, r, :], in0=bc[:, 256:512], scalar=vsT[:, r, j:j + 1], in1=cb[:, r, :],
                                                                                 op0=ALU.mult, op1=ALU.add), [bc, vsT, cb], [cb])
                                  k.dma("sp", Cs[l, j, i], cb[:, :, :], [cb], [], pwrites=[out_t["Cs"]])
                                  yield "j"
                              for r in range(2):
                                  k.mm(pt32, pt32[0:NS, 128 * r:128 * (r + 1)], cqT, cqT[:, r, :], cst, ident, start=True, stop=True, last=(r == 1), transpose=True)
                              k.op("dve", lambda: V.tensor_copy(out=cq_tok[:, :].rearrange("p (a r) -> p r a", r=2),
                                                                in_=pt32[0:NS, 0:256].rearrange("p (r a) -> p r a", r=2)), [pt32], [cq_tok])
                              tmpv = wk[2]
                              k.op("dve", lambda: V.tensor_scalar(out=tmpv[0:NS, 0:256], in0=v_s[:, :], scalar1=s_[:, 0:1], scalar2=None, op0=ALU.mult), [v_s, s_], [tmpv])
                              num = wk[3]
                              k.op("dve", lambda: V.scalar_tensor_tensor(out=num[0:NS, 0:256], in0=cq_tok[:, :], scalar=S_["wi"][:, i:i + 1], in1=tmpv[0:NS, 0:256],
                                                                         op0=ALU.mult, op1=ALU.add), [cq_tok, S_["wi"], tmpv], [num])
                              k.op("dve", lambda: V.scalar_tensor_tensor(out=den[:, 0:1], in0=nq[:, 0:1], scalar=S_["wi"][:, i:i + 1], in1=s_[:, 0:1],
                                                                         op0=ALU.mult, op1=ALU.add), [nq, S_["wi"], s_], [den])
                              k.op("dve", lambda: V.tensor_scalar(out=den[:, 1:2], in0=den[:, 0:1], scalar1=-1.0, scalar2=None, op0=ALU.mult), [den], [den])
                              k.op("dve", lambda: V.tensor_tensor(out=den[:, 0:1], in0=den[:, 0:1], in1=den[:, 1:2], op=ALU.max), [den], [den])
                              k.op("dve", lambda: V.tensor_tensor(out=den[:, 0:1], in0=den[:, 0:1], in1=S_["enm"][:, i:i + 1], op=ALU.max), [den, S_["enm"]], [den])
                              k.op("dve", lambda: V.reciprocal(out=rec[:, 0:1], in_=den[:, 0:1]), [den], [rec])
                              k.op("dve", lambda: V.scalar_tensor_tensor(out=og[8][0:NS, :], in0=num[0:NS, 0:256], scalar=rec[:, 0:1], in1=og[8][0:NS, :],
                                                                         op0=ALU.mult, op1=ALU.mult), [num, rec, og[8]], [og[8]])
                              sq = wk[4]
                              d1 = sm1[3]
                              k.op("act", lambda: A.activation(out=sq[0:NS, 0:256], in_=og[8][0:NS, :], func=AF.Square), [og[8]], [sq])
                              k.op("dve", lambda: V.tensor_reduce(out=d1[0:NS, 1:2], in_=sq[0:NS, 0:256], axis=AX.X, op=ALU.add), [sq], [d1])
                              rstd_from_ss(NS, d1[0:NS, 1:2], d1, float(HD), rr, rr[0:NS, 8:9], d1, d1[0:NS, 2:3])
                              k.op("dve", lambda: V.tensor_scalar(out=tmpv[0:NS, 0:256], in0=qk_s[:, 256:512], scalar1=S_["wa"][:, i:i + 1], scalar2=None, op0=ALU.mult),
                                   [qk_s, S_["wa"]], [tmpv])
                              k.op("dve", lambda: V.scalar_tensor_tensor(out=n_s[:, :], in0=n_s[:, :], scalar=S_["wi"][:, i:i + 1], in1=tmpv[0:NS, 0:256],
                                                                         op0=ALU.mult, op1=ALU.add), [n_s, S_["wi"], tmpv], [n_s])
                              k.dma("sp", nso[l, :, i * HD:(i + 1) * HD], n_s[:, :], [n_s], [], pwrites=[out_t["nso"]])


                      def passA():
                          kg = wkb[1]
                          for c in range(8):
                              k.op("dve", lambda: V.tensor_scalar(out=kg[:, 0:256], in0=ktok[:, c, :], scalar1=G_["GA"][:, c, i:i + 1], scalar2=None, op0=ALU.mult),
                                   [ktok, G_["GA"]], [kg])
                              for dt_ in range(2):
                                  k.mm(pupd, pupd[:, dt_, :], kg, kg[:, dt_ * 128:(dt_ + 1) * 128], vtok, vtok[:, c, :], start=True, stop=True, last=(dt_ == 1))
                              for dt_ in range(2):
                                  k.mm(psml, psml[:, 84 + dt_:85 + dt_], kg, kg[:, dt_ * 128:(dt_ + 1) * 128], onesb, onesb[:, :], start=True, stop=True, last=(dt_ == 1))
                              k.op("dve", lambda: V.scalar_tensor_tensor(out=CT32[i][:, :, :].rearrange("p a b -> p (a b)"),
                                                                         in0=CT32[i][:, :, :].rearrange("p a b -> p (a b)"), scalar=G_["DECA"][:, c, i:i + 1],
                                                                         in1=pupd[:, :, :].rearrange("p a b -> p (a b)"), op0=ALU.mult, op1=ALU.add),
                                   [CT32[i], G_["DECA"], pupd], [CT32[i]])
                              k.op("dve", lambda: V.scalar_tensor_tensor(out=n32[i][:, :], in0=n32[i][:, :], scalar=G_["DECA"][:, c, i:i + 1], in1=psml[:, 84:86],
                                                                         op0=ALU.mult, op1=ALU.add), [n32[i], G_["DECA"], psml], [n32[i]])
                          k.dma("sp", cinC[l][i][:, 0:512], CT32[i][:, :, :].rearrange("p a b -> p (a b)"), [CT32[i]], [], pwrites=[cinC_t[l][i]])
                          k.dma("sp", cinC[l][i][:, 512:514], n32[i][:, :], [n32[i]], [], pwrites=[cinC_t[l][i]])
                          k.cc(cinC[l][i], coutC[l][i], [cinC_t[l][i]], [coutC_t[l][i]])
                      def loadState():
                          k.dma("sp", CT32[i][:, :, :].rearrange("p a b -> p (a b)"), coutC[l][i][0:128, 0:512], [coutC_t[l][i]], [CT32[i]])
                          k.dma("sp", n32[i][:, :], coutC[l][i][0:128, 512:514], [coutC_t[l][i]], [n32[i]])
                          k.op("dve", lambda: V.tensor_scalar(out=CT32[i][:, :, :], in0=CT32[i][:, :, :], scalar1=msk_bc[:, 0:1], scalar2=None, op0=ALU.mult),
                               [CT32[i], msk_bc], [CT32[i]])
                          k.op("dve", lambda: V.tensor_scalar(out=n32[i][:, :], in0=n32[i][:, :], scalar1=msk_bc[:, 0:1], scalar2=None, op0=ALU.mult),
                               [n32[i], msk_bc], [n32[i]])
                      def drain(g):
                          for _ in g:
                              pass

                      if COLL:
                          passA()
                          stage(7)
                          gS = sampleM()
                          next(gS)
                          for _ in range(4):
                              next(gS)
                          if i == 0:
                              chainB()
                          loadState()
                          gB = passB()
                          next(gB)
                          for c in range(8):
                              assert next(gB) == "a"
                              next(gS, None)
                              r_ = next(gB)
                              if r_ == "b":
                                  if c % 2 == 0:
                                      next(gS, None)
                                  assert next(gB) == "c"
                          drain(gB)
                          drain(gS)
                      else:
                          drain(passB())
                          stage(7)
                          if half == 0:
                              drain(sampleM())
                      stage(8)
                      k.dma("sp", gmh_bc[:, :], gmh[l, i * 256:(i + 1) * 256].partition_broadcast(128), [], [gmh_bc])
                      k.dma("sp", gcmv_bc[:, :], gcmv[l, i * 256:(i + 1) * 256].partition_broadcast(128), [], [gcmv_bc])
                      wst = wk[5]
                      k.dma("sp", wst[:, 0:128], wsT[l, i], [], [wst])
                      k.op("dve", lambda: V.tensor_tensor(out=wsTb[:, i, :], in0=wst[:, 0:128], in1=tri, op=ALU.mult), [wst, cst], [wsTb])
                      unitC, ujC = wacquire(("inp", l, 4 * i + 2))
                      unitD, ujD = wacquire(("inp", l, 4 * i + 3))
                      pairs = [(pj[0], pj[1]), (pupd512, psb)]

                      def projCD(ti):
                          tt_, P_, tok0_ = tiles[ti]
                          pC_, pD_ = pairs[ti % 2]
                          proj_tok(unitC, tt_, P_, tok0_, pC_)
                          proj_tok(unitD, tt_, P_, tok0_, pD_)

                      projCD(0)
                      for ti, (tt, P, tok0) in enumerate(tiles):
                          pC, pD = pairs[ti % 2]
                          if ti + 1 < len(tiles):
                              projCD(ti + 1)
                          sz, gv, vn, mg = wk[0], wk[1], wk[2], wkb[0]
                          gu, szc, hc = wk[4], wk[5], wkb[0]
                          k.op("act", lambda: A.activation(out=gv[0:P, 0:256], in_=pC[0:P, 256:512], func=AF.Gelu), [pC], [gv])
                          k.op("act", lambda: A.activation(out=gu[0:P, 0:256], in_=pD[0:P, 0:256], func=AF.Gelu), [pD], [gu])
                          k.op("act", lambda: A.activation(out=sz[0:P, 0:256], in_=pC[0:P, 0:256], func=AF.Silu), [pC], [sz])
                          k.op("act", lambda: A.activation(out=szc[0:P, 0:256], in_=pD[0:P, 256:512], func=AF.Silu), [pD], [szc])
                          st6, mv = sm1[6], sm1[7]
                          k.op("dve", lambda: V.bn_stats(out=st6[0:P, 0:6], in_=gv[0:P, 0:256]), [gv], [st6])
                          k.op("dve", lambda: V.bn_aggr(out=mv[0:P, 0:2], in_=st6[0:P, 0:6]), [st6], [mv])
                          k.op("dve", lambda: V.tensor_scalar(out=mv[0:P, 2:3], in0=mv[0:P, 1:2], scalar1=EPS, scalar2=None, op0=ALU.add), [mv], [mv])
                          k.op("act", lambda: A.activation(out=mv[0:P, 2:3], in_=mv[0:P, 2:3], func=AF.Sqrt), [mv], [mv])
                          k.op("dve", lambda: V.tensor_tensor(out=sz[0:P, 0:256], in0=sz[0:P, 0:256], in1=gmh_bc[0:P, :], op=ALU.mult), [sz, gmh_bc], [sz])
                          k.op("dve", lambda: V.scalar_tensor_tensor(out=mg[0:P, 0:256], in0=og[tt][0:P, :], scalar=rr[0:P, tt:tt + 1], in1=sz[0:P, 0:256],
                                                                     op0=ALU.mult, op1=ALU.mult), [og[tt], rr, sz], [mg])
                          for dt_ in range(2):
                              k.mm(ptb, ptb[:, dt_ * 128:dt_ * 128 + P], mg, mg[0:P, dt_ * 128:(dt_ + 1) * 128], identb, identb[0:P, 0:P],
                                   start=True, stop=True, last=(dt_ == 1), transpose=True)
                          k.op("dve", lambda: V.reciprocal(out=mv[0:P, 3:4], in_=mv[0:P, 2:3]), [mv], [mv])
                          k.op("dve", lambda: V.tensor_scalar(out=gv[0:P, 0:256], in0=gv[0:P, 0:256], scalar1=mv[0:P, 0:1], scalar2=mv[0:P, 3:4],
                                                              op0=ALU.subtract, op1=ALU.mult), [gv, mv], [gv])
                          smx = wk[3]
                          if tt < 8:
                              vnb = wkb[1]
                              k.op("dve", lambda: V.tensor_tensor(out=vnb[:, 0:256], in0=gv[:, 0:256], in1=gcmv_bc[:, :], op=ALU.mult), [gv, gcmv_bc], [vnb])
                              k.mm(pintra, pintra[:, :], wsTb, wsTb[:, i, :], vnb, vnb[:, 0:256], start=True, stop=True)
                          k.op("act", lambda: A.copy(out=mT[:, 2 * i:2 * i + 2, tok0:tok0 + P],
                                                     in_=ptb[:, 0:256].rearrange("p (a b) -> p a b", a=2)[:, :, 0:P]), [ptb], [mT])
                          if tt < 8:
                              k.op("act", lambda: A.activation(out=smx[:, 0:256], in_=pintra[:, :], func=AF.Identity, bias=bsT_sb[:, l, i:i + 1]),
                                   [pintra, bsT_sb], [smx])
                          else:
                              k.op("dve", lambda: V.tensor_tensor(out=vn[0:NS, 0:256], in0=gv[0:NS, 0:256], in1=gcmv_bc[0:NS, :], op=ALU.mult), [gv, gcmv_bc], [vn])
                              k.dma("sp", vro[l, :, i * 256:(i + 1) * 256], vn[0:NS, 0:256], [vn], [], pwrites=[out_t["vro"]])
                              k.op("dve", lambda: V.tensor_scalar(out=smx[0:NS, 0:256], in0=vn[0:NS, 0:256], scalar1=ws00_bc[0:NS, l, i:i + 1],
                                                                  scalar2=ws00_bc[0:NS, l, 4 + i:5 + i], op0=ALU.mult, op1=ALU.add), [vn, ws00_bc], [smx])
                          k.op("dve", lambda: V.tensor_tensor(out=gu[0:P, 0:256], in0=gu[0:P, 0:256], in1=szc[0:P, 0:256], op=ALU.mult), [gu, szc], [gu])
                          k.op("dve", lambda: V.tensor_tensor(out=hc[0:P, 0:256], in0=gu[0:P, 0:256], in1=smx[0:P, 0:256], op=ALU.mult), [gu, smx], [hc])
                          for dt_ in range(2):
                              k.mm(ptb, ptb[:, 512 + dt_ * 128:512 + dt_ * 128 + P], hc, hc[0:P, dt_ * 128:(dt_ + 1) * 128], identb, identb[0:P, 0:P],
                                   start=True, stop=True, last=(dt_ == 1), transpose=True)
                          k.op("act", lambda: A.copy(out=mT[:, 8 + 2 * i:8 + 2 * i + 2, tok0:tok0 + P],
                                                     in_=ptb[:, 512:768].rearrange("p (a b) -> p a b", a=2)[:, :, 0:P]), [ptb], [mT])
                      wrelease(ujC)
                      wrelease(ujD)
                      if EARLY_ADA and l == 0:
                          ada_unit(1, 2 * i, adaT2)
                          ada_unit(1, 2 * i + 1, adaT2)

                  stage(9)
                  for j in range(4):
                      unitG, ujG = wacquire(("ada", l, 8 + j))
                      k.dma("sp", gate_bc[:, :], bgate[l, j * 512:(j + 1) * 512].partition_broadcast(128), [], [gate_bc])
                      for kc in range(KC):
                          k.mm(pj[0], pj[0][:, :], scTb, scTb[:, kc, :], unitG, unitG[:, kc, :], start=(kc == 0), stop=(kc == KC - 1))
                      if half == 0:
                          for kc in range(KC):
                              k.mm(pj[1], pj[1][:, :], scTs, scTs[:, kc, :], unitG, unitG[:, kc, :], start=(kc == 0), stop=(kc == KC - 1))
                          k.op("dve", lambda: V.tensor_tensor(out=gate_s[0:NS, :], in0=pj[1][0:NS, :], in1=gate_bc[0:NS, :], op=ALU.add), [pj[1], gate_bc], [gate_s])
                      k.op("dve", lambda: V.tensor_tensor(out=gate_bc[:, :], in0=pj[0][:, :], in1=gate_bc[:, :], op=ALU.add), [pj[0], gate_bc], [gate_bc])
                      wrelease(ujG)
                      unitO, ujO = wacquire(("out", l, j))
                      def oload(ti_):
                          tt_, P_, tok0_ = tiles[ti_]
                          st_ = xin_t(tt_, half)
                          k.dma("sp", wk[ti_ % 2][0:P_, :], xin(tt_, half)[:, j * 512:(j + 1) * 512], [st_] if st_ else [], [wk[ti_ % 2]])

                      for ti, (tt, P, tok0) in enumerate(tiles):
                          pjt = pj[ti % 2]
                          for kc in range(KC):
                              k.mm(pjt, pjt[:, :], mT, mT[:, kc, tok0:tok0 + 128], unitO, unitO[:, kc, :], start=(kc == 0), stop=(kc == KC - 1))
                          xo = wk[ti % 2]
                          tg = wk[2 + ti % 2]
                          if ti == 0:
                              oload(0)
                          if ti + 1 < len(tiles):
                              oload(ti + 1)
                          gsrc = gate_bc if tt < 8 else gate_s
                          k.op("dve", lambda: V.tensor_tensor(out=tg[0:P, :], in0=pjt[0:P, :], in1=gsrc[0:P, :], op=ALU.mult), [pjt, gsrc], [tg])
                          k.op("dve", lambda: V.tensor_tensor(out=tg[0:P, :], in0=tg[0:P, :], in1=xo[0:P, :], op=ALU.add), [tg, xo], [tg])
                          row0 = half * 1024 + tt * 128 if tt < 8 else NT
                          dt_t = xs_t[(xout_nm, half * 8 + tt if tt < 8 else NT // 128)]
                          k.dma("sp", xout[row0:row0 + P, j * 512:(j + 1) * 512], tg[0:P, :], [tg], [], pwrites=[dt_t])
                      wrelease(ujO)


        except StopBuild:
            pass
        gf = [wk[0], wk[1], wk[2], wk[3]]
        for j in range(4):
            k.dma("sp", gf[j][:, :], gfin[0, j * 512:(j + 1) * 512].partition_broadcast(128), [], [gf[j]])
        all_tiles = [(h_ * 8 + c, 128, h_ * 1024 + c * 128) for h_ in range(NHALF) for c in range(8)] + [(NT // 128, NS, NT)]
        def fload(ti):
            gt__, P_, row0_ = all_tiles[ti]
            xb_ = xbs[ti % 3]
            for ap_, off_ in xb_.parts:
                w_ = ap_.shape[1]
                k.dma("sp", ap_[0:P_, :], xs2[row0_:row0_ + P_, off_:off_ + w_], [xs_t[("xs2", gt__)]], xb_.tls)

        fload(0)
        fload(1)
        for ti, (gt_, P, row0) in enumerate(all_tiles):
            if ti + 2 < len(all_tiles):
                fload(ti + 2)
            xb = xbs[ti % 3]
            cover = xb.tls
            ssq = sm1[0]
            for j4 in range(4):
                sqw = wk[4 + j4 % 2]
                k.op("act", lambda: A.activation(out=sqw[0:P, :], in_=xb.sl(P, j4 * 512, (j4 + 1) * 512), func=AF.Square), cover, [sqw])
                k.op("dve", lambda: V.tensor_reduce(out=ssq4[0:P, j4:j4 + 1], in_=sqw[0:P, :], axis=AX.X, op=ALU.add), [sqw], [ssq4])
            k.op("dve", lambda: V.tensor_reduce(out=ssq[0:P, 0:1], in_=ssq4[0:P, :], axis=AX.X, op=ALU.add), [ssq4], [ssq])
            rstd_from_ss(P, ssq[0:P, 0:1], ssq, float(D), sm1[1], sm1[1][0:P, 0:1], sm1[2], sm1[2][0:P, 0:1])
            for j in range(4):
                xs_ = xb.sl(P, j * 512, (j + 1) * 512)
                k.op("dve", lambda: V.scalar_tensor_tensor(out=xs_, in0=xs_, scalar=sm1[1][0:P, 0:1], in1=gf[j][0:P, :], op0=ALU.mult, op1=ALU.mult),
                     cover + [sm1[1], gf[j]], cover)
            dst = yp[row0:row0 + P, :] if gt_ < NT // 128 else ys[:, :]
            dst_t = out_t["yp"] if gt_ < NT // 128 else out_t["ys"]
            for ap_, off_ in xb.parts:
                w_ = ap_.shape[1]
                k.dma("sp", dst[:, off_:off_ + w_], ap_[0:P, :], cover, [], pwrites=[dst_t])
        for ds in k.dsems:
            k._wait("sp", (ds[0], ds[1]))
    return nc


_CACHE = {}


def _consts():
    c = np.zeros((5, 128, 128), np.float32)
    c[0] = np.eye(128)
    c[1] = 1.0
    s = np.arange(128)[:, None]
    t = np.arange(128)[None, :]
    c[2] = (s <= t)
    c[3] = np.where(s <= t, 0.0, -30000.0)
    c[4][127, :] = 1.0
    return c


def kernel(x_prompt, x_sample, state_C, state_n, state_m, c_prompt, c_sample,
           g_norm, w_ada, b_ada, w_in, b_igate, b_fgate, g_mh, g_cmv, w_s, b_s, w_out, g_final):
    f = lambda a: np.ascontiguousarray(np.asarray(a), dtype=np.float32)
    x_prompt, x_sample, state_C, state_n, state_m = map(f, (x_prompt, x_sample, state_C, state_n, state_m))
    c_prompt, c_sample, g_norm, w_ada, b_ada, w_in = map(f, (c_prompt, c_sample, g_norm, w_ada, b_ada, w_in))
    b_igate, b_fgate, g_mh, g_cmv, w_s, b_s, w_out, g_final = map(f, (b_igate, b_fgate, g_mh, g_cmv, w_s, b_s, w_out, g_final))
    if "nc" not in _CACHE:
        _CACHE["nc"] = build_nc()
    nc = _CACHE["nc"]
    offs = dict(q=0, k=1024, v=2048, o=3072, z=4096, gi=5120, gf=5124, u=5128, vc=6152, zc=7176)
    cols = []
    for i in range(4):
        for a, b in (("q", "k"), ("v", "o"), ("z", "vc"), ("u", "zc")):
            cols += list(range(offs[a] + i * 256, offs[a] + (i + 1) * 256))
            cols += list(range(offs[b] + i * 256, offs[b] + (i + 1) * 256))
    cols = np.array(cols)
    w_inp = np.ascontiguousarray(w_in[:, :, cols])
    wgT = np.ascontiguousarray(w_in[:, :, 5120:5128].reshape(2, KC, 128, 8).transpose(0, 2, 1, 3))
    badaT = np.ascontiguousarray(b_ada[:, 0:4096].reshape(2, 32, 128).transpose(0, 2, 1))
    bgate = np.ascontiguousarray(b_ada[:, 4096:6144])
    gnT = np.ascontiguousarray(g_norm.reshape(2, KC, 128).transpose(0, 2, 1))
    big = np.ascontiguousarray(np.concatenate([b_igate, b_fgate], axis=1))
    wsT = np.ascontiguousarray(w_s.transpose(0, 1, 3, 2))
    bsT = np.ascontiguousarray(b_s.transpose(0, 2, 1))
    ws00 = np.ascontiguousarray(np.concatenate([w_s[:, :, 0, 0], b_s[:, :, 0]], axis=1))
    consts = _consts()
    in_maps = []
    for c in range(NCORES):
        b = c // 2
        s = c % 2
        if NHALF == 2:
            xp = x_prompt[b]
        else:
            xp = x_prompt[b, s * 1024:(s + 1) * 1024]
        sl = slice(c * NS, (c + 1) * NS)
        in_maps.append(dict(
            xp=np.ascontiguousarray(xp), xsm=np.ascontiguousarray(x_sample[sl, 0, :]),
            cvec=np.ascontiguousarray(np.concatenate([c_prompt[b:b + 1], c_sample[sl]], axis=0)),
            sC=np.ascontiguousarray(state_C[:, sl].reshape(2, NS, NHD, 128, 2, HD)),
            sn=np.ascontiguousarray(state_n[:, sl].reshape(2, NS, NHD * HD)),
            smm=np.ascontiguousarray(state_m[:, sl]),
            w_ada=w_ada, w_inp=w_inp, w_out=w_out, wgT=wgT, badaT=badaT, bgate=bgate, gnT=gnT, big=big,
            gmh=g_mh, gcmv=g_cmv, wsT=wsT, bsT=bsT, ws00=ws00, gfin=g_final.reshape(1, D), consts=consts, msk=np.full((1, 1), float(s), np.float32),
        ))
    res = run_bass_kernel_spmd(nc, in_maps, core_ids=list(range(NCORES)))
    R = res.results
    B = 4
    y_prompt = np.zeros((B, 2048, D), np.float32)
    y_sample = np.zeros((128, 1, D), np.float32)
    Cp = np.zeros((2, B, NHD, HD, HD), np.float32)
    npr = np.zeros((2, B, NHD, HD), np.float32)
    mp = np.zeros((2, B, NHD), np.float32)
    Cs = np.zeros((2, 128, NHD, HD, HD), np.float32)
    ns = np.zeros((2, 128, NHD, HD), np.float32)
    ms = np.zeros((2, 128, NHD), np.float32)
    vr = np.zeros((2, 128, 1, 1024), np.float32)
    for c in range(NCORES):
        b = c // 2
        s = c % 2
        r = R[c]
        sl = slice(c * NS, (c + 1) * NS)
        if NHALF == 2:
            if s == 0:
                y_prompt[b] = r["yp"]
        else:
            y_prompt[b, s * 1024:(s + 1) * 1024] = r["yp"]
        y_sample[sl, 0, :] = r["ys"]
        if (NHALF == 2 and s == 0) or (NHALF == 1 and s == 1):
            cp = r["Cp"].transpose(0, 1, 3, 2, 4).reshape(2, NHD, HD, HD)
            Cp[:, b] = cp.transpose(0, 1, 3, 2)
            npr[:, b] = r["npo"].transpose(0, 1, 3, 2).reshape(2, NHD, HD)
            mp[:, b] = r["mpo"]
        Cs[:, sl] = r["Cs"].reshape(2, NS, NHD, HD, HD)
        ns[:, sl] = r["nso"].reshape(2, NS, NHD, HD)
        ms[:, sl] = r["mso"]
        vr[:, sl, 0, :] = r["vro"]
    return (y_prompt, y_sample, Cp, npr, mp, Cs, ns, ms, vr)
```
